# Optimizing a Trainium2 kernel written in Bass

```python
import math
import jax, jax.numpy as jnp
from jax import lax
import numpy as np

D_MODEL = 1024
BATCH = 8
SEQ = 2048
DEPTH = 2

HEAD_DIM = 64
A_HEADS = 4
DILATED_PATTERNS = ((128, 1), (512, 4), (2048, 16))
WIN_BLOCK = 64
B_HEADS = 8
MLA_Q_LORA = 384
MLA_KV_LORA = 256
MLA_NOPE = 64
MLA_ROPE = 32
MLA_V = 64
ROPE_THETA = 10000.0
C_HEADS = 4
DIFF_QK = 32
DIFF_V = 64
D_FF = 2816
QUERY_BLOCK = 128
EPS = 1e-6
NEG_INF = -1e30

A_COLS = 3 * A_HEADS * HEAD_DIM
B_COLS = MLA_Q_LORA + MLA_KV_LORA + MLA_ROPE
C_COLS = 2 * (C_HEADS * 2 * DIFF_QK) + C_HEADS * DIFF_V
N_IN = A_COLS + B_COLS + C_COLS
D_MIX = A_HEADS * HEAD_DIM + B_HEADS * MLA_V + C_HEADS * DIFF_V

kernel_name = "hybrid_dilated_mla_diff_macaron_encoder"


def rmsnorm(x, gain):
    xf = x.astype(jnp.float32)
    y = xf * lax.rsqrt(jnp.mean(xf * xf, axis=-1, keepdims=True) + EPS)
    return y.astype(x.dtype) * gain


def swiglu(h, w_gate, w_up, w_down):
    return (jax.nn.silu(h @ w_gate) * (h @ w_up)) @ w_down


def alibi_slopes():
    n = A_HEADS + C_HEADS
    s = jnp.asarray([2.0 ** (-8.0 * (i + 1) / n) for i in range(n)], dtype=jnp.float32)
    return s[0::2], s[1::2]


def rope(t, pos):
    half = t.shape[-1] // 2
    inv = ROPE_THETA ** (-jnp.arange(half, dtype=jnp.float32) / half)
    ang = pos.astype(jnp.float32)[..., None, None] * inv
    cos, sin = jnp.cos(ang), jnp.sin(ang)
    tf = t.astype(jnp.float32)
    t1, t2 = tf[..., :half], tf[..., half:]
    return jnp.concatenate([t1 * cos - t2 * sin, t2 * cos + t1 * sin], axis=-1).astype(t.dtype)


def dilated_branch(q, k, v, pos, slopes, dilation, radius):
    B, S, H, E = q.shape
    L = S // dilation

    def to_res(t):
        return t.reshape(B, L, dilation, H, E).transpose(0, 2, 3, 1, 4)

    qr, kr, vr = to_res(q), to_res(k), to_res(v)
    pr = pos.reshape(B, L, dilation).transpose(0, 2, 1)
    nb = -(-L // WIN_BLOCK)
    Lp = nb * WIN_BLOCK
    K = WIN_BLOCK + 2 * radius
    qr = jnp.pad(qr, ((0, 0), (0, 0), (0, 0), (0, Lp - L), (0, 0)))
    kpad = ((0, 0), (0, 0), (0, 0), (radius, radius + Lp - L), (0, 0))
    kr, vr = jnp.pad(kr, kpad), jnp.pad(vr, kpad)
    pq = jnp.pad(pr, ((0, 0), (0, 0), (0, Lp - L)))
    pk = jnp.pad(pr, ((0, 0), (0, 0), (radius, radius + Lp - L)))
    slab = jnp.arange(nb)[:, None] * WIN_BLOCK + jnp.arange(K)[None, :]
    ks = jnp.take(kr, slab, axis=3)
    vs = jnp.take(vr, slab, axis=3)
    pks = jnp.take(pk, slab, axis=2)
    qb = qr.reshape(B, dilation, H, nb, WIN_BLOCK, E)
    pqb = pq.reshape(B, dilation, nb, WIN_BLOCK)
    off = jnp.arange(K)[None, :] - radius - jnp.arange(WIN_BLOCK)[:, None]
    kabs = slab - radius
    valid = (jnp.abs(off) <= radius)[None] & ((kabs >= 0) & (kabs < L))[:, None, :]
    sc = jnp.einsum('bdhnqe,bdhnke->bdhnqk', qb, ks).astype(jnp.float32) * (E ** -0.5)
    dist = jnp.abs(pqb[..., :, None] - pks[..., None, :]).astype(jnp.float32)
    sc = sc - slopes[None, None, :, None, None, None] * dist[:, :, None]
    sc = jnp.where(valid, sc, NEG_INF)
    m = jnp.max(sc, axis=-1)
    p = jnp.exp(sc - m[..., None])
    s = jnp.sum(p, axis=-1)
    o = jnp.einsum('bdhnqk,bdhnke->bdhnqe', p, vs.astype(jnp.float32)) / s[..., None]
    o = o.reshape(B, dilation, H, Lp, E)[:, :, :, :L].transpose(0, 3, 1, 2, 4).reshape(B, S, H, E)
    m = m.reshape(B, dilation, H, Lp)[..., :L].transpose(0, 3, 1, 2).reshape(B, S, H)
    s = s.reshape(B, dilation, H, Lp)[..., :L].transpose(0, 3, 1, 2).reshape(B, S, H)
    return o, m, s


def dilated_attention(q, k, v, pos, slopes):
    outs = [dilated_branch(q, k, v, pos, slopes, d, w // 2 // d) for (w, d) in DILATED_PATTERNS]
    m_all = jnp.max(jnp.stack([m for (_, m, _) in outs]), axis=0)
    wts = [s * jnp.exp(m - m_all) for (_, m, s) in outs]
    num = sum(w[..., None] * o for w, (o, _, _) in zip(wts, outs))
    return num / sum(wts)[..., None]


def mla_attention(q, k, v):
    B, H, S, Dk = q.shape
    nq = S // QUERY_BLOCK
    scale = Dk ** -0.5
    qb = q.reshape(B, H, nq, QUERY_BLOCK, Dk).transpose(2, 0, 1, 3, 4)
    vf = v.astype(jnp.float32)

    def one(qi):
        sc = jnp.einsum('bhqe,bhke->bhqk', qi, k).astype(jnp.float32) * scale
        return jnp.einsum('bhqk,bhkv->bhqv', jax.nn.softmax(sc, axis=-1), vf)

    o = lax.map(one, qb)
    return o.transpose(1, 0, 3, 2, 4).reshape(B, S, H, v.shape[-1])


def diff_attention(q, k, v, pos, slopes, lam):
    B, H, _, S, E = q.shape
    nq = S // QUERY_BLOCK
    scale = E ** -0.5
    qb = q.reshape(B, H, 2, nq, QUERY_BLOCK, E).transpose(3, 0, 1, 2, 4, 5)
    pqb = pos.reshape(B, nq, QUERY_BLOCK).transpose(1, 0, 2)
    vf = v.astype(jnp.float32)

    def one(args):
        qi, pi = args
        sc = jnp.einsum('bhcqe,bhcke->bhcqk', qi, k).astype(jnp.float32) * scale
        dist = jnp.abs(pi[:, :, None] - pos[:, None, :]).astype(jnp.float32)
        sc = sc - slopes[None, :, None, None, None] * dist[:, None, None]
        a = jax.nn.softmax(sc, axis=-1)
        return jnp.einsum('bhqk,bhkv->bhqv', a[:, :, 0] - lam * a[:, :, 1], vf)

    o = lax.map(one, (qb, pqb))
    return o.transpose(1, 0, 3, 2, 4).reshape(B, S, H, v.shape[-1])


def token_mixing(h, pos, w_in, mla_q_norm, mla_w_uq, mla_kv_norm, mla_w_ukv,
                 lq1, lk1, lq2, lk2, diff_head_norm, w_out, layer_idx):
    B, S, _ = h.shape
    slopes_a, slopes_c = alibi_slopes()
    proj = h @ w_in
    a_part, b_part, c_part = jnp.split(proj, [A_COLS, A_COLS + B_COLS], axis=-1)

    qa, ka, va = [t.reshape(B, S, A_HEADS, HEAD_DIM) for t in jnp.split(a_part, 3, axis=-1)]
    out_a = dilated_attention(qa, ka, va, pos, slopes_a).astype(h.dtype)

    c_q, c_kv, k_pe = jnp.split(b_part, [MLA_Q_LORA, MLA_Q_LORA + MLA_KV_LORA], axis=-1)
    qfull = (rmsnorm(c_q, mla_q_norm) @ mla_w_uq).reshape(B, S, B_HEADS, MLA_NOPE + MLA_ROPE)
    q_b = jnp.concatenate([qfull[..., :MLA_NOPE], rope(qfull[..., MLA_NOPE:], pos)], axis=-1)
    kv = (rmsnorm(c_kv, mla_kv_norm) @ mla_w_ukv).reshape(B, S, B_HEADS, MLA_NOPE + MLA_V)
    k_rot = jnp.broadcast_to(rope(k_pe[:, :, None, :], pos), (B, S, B_HEADS, MLA_ROPE))
    k_b = jnp.concatenate([kv[..., :MLA_NOPE], k_rot], axis=-1)
    v_b = kv[..., MLA_NOPE:]
    out_b = mla_attention(q_b.transpose(0, 2, 1, 3), k_b.transpose(0, 2, 1, 3),
                          v_b.transpose(0, 2, 1, 3)).astype(h.dtype)

    nqk = C_HEADS * 2 * DIFF_QK
    q_c, k_c, v_c = jnp.split(c_part, [nqk, 2 * nqk], axis=-1)
    q_c = q_c.reshape(B, S, C_HEADS, 2, DIFF_QK).transpose(0, 2, 3, 1, 4)
    k_c = k_c.reshape(B, S, C_HEADS, 2, DIFF_QK).transpose(0, 2, 3, 1, 4)
    v_c = v_c.reshape(B, S, C_HEADS, DIFF_V).transpose(0, 2, 1, 3)
    lam_init = 0.8 - 0.6 * math.exp(-0.3 * layer_idx)
    lam = (jnp.exp(jnp.sum(lq1.astype(jnp.float32) * lk1.astype(jnp.float32)))
           - jnp.exp(jnp.sum(lq2.astype(jnp.float32) * lk2.astype(jnp.float32))) + lam_init)
    o_c = diff_attention(q_c, k_c, v_c, pos, slopes_c, lam).astype(h.dtype)
    out_c = rmsnorm(o_c, diff_head_norm) * (1.0 - lam_init)

    mixed = jnp.concatenate([out_a.reshape(B, S, -1), out_b.reshape(B, S, -1),
                             out_c.reshape(B, S, -1)], axis=-1)
    return mixed @ w_out


def setup_inputs(seed: int = 0) -> dict:
    key = jax.random.key(seed)
    ks = iter(jax.random.split(key, 32))
    f32 = jnp.float32

    def dense(shape, fan_in):
        return jax.random.normal(next(ks), shape, f32) * (fan_in ** -0.5)

    def gain(shape):
        return 1.0 + 0.02 * jax.random.normal(next(ks), shape, f32)

    return {
        "x": jax.random.normal(next(ks), (BATCH, SEQ, D_MODEL), f32),
        "positions": jnp.broadcast_to(jnp.arange(SEQ, dtype=jnp.int32), (BATCH, SEQ)),
        "ffn1_norm": gain((DEPTH, D_MODEL)),
        "ffn1_w_gate": dense((DEPTH, D_MODEL, D_FF), D_MODEL),
        "ffn1_w_up": dense((DEPTH, D_MODEL, D_FF), D_MODEL),
        "ffn1_w_down": dense((DEPTH, D_FF, D_MODEL), D_FF),
        "mix_norm": gain((DEPTH, D_MODEL)),
        "w_in": dense((DEPTH, D_MODEL, N_IN), D_MODEL),
        "mla_q_norm": gain((DEPTH, MLA_Q_LORA)),
        "mla_w_uq": dense((DEPTH, MLA_Q_LORA, B_HEADS * (MLA_NOPE + MLA_ROPE)), MLA_Q_LORA),
        "mla_kv_norm": gain((DEPTH, MLA_KV_LORA)),
        "mla_w_ukv": dense((DEPTH, MLA_KV_LORA, B_HEADS * (MLA_NOPE + MLA_V)), MLA_KV_LORA),
        "diff_lambda_q1": 0.1 * jax.random.normal(next(ks), (DEPTH, DIFF_QK), f32),
        "diff_lambda_k1": 0.1 * jax.random.normal(next(ks), (DEPTH, DIFF_QK), f32),
        "diff_lambda_q2": 0.1 * jax.random.normal(next(ks), (DEPTH, DIFF_QK), f32),
        "diff_lambda_k2": 0.1 * jax.random.normal(next(ks), (DEPTH, DIFF_QK), f32),
        "diff_head_norm": gain((DEPTH, DIFF_V)),
        "w_out": dense((DEPTH, D_MIX, D_MODEL), D_MIX),
        "ffn2_norm": gain((DEPTH, D_MODEL)),
        "ffn2_w_gate": dense((DEPTH, D_MODEL, D_FF), D_MODEL),
        "ffn2_w_up": dense((DEPTH, D_MODEL, D_FF), D_MODEL),
        "ffn2_w_down": dense((DEPTH, D_FF, D_MODEL), D_FF),
        "final_norm": gain((D_MODEL,)),
    }


def reference(x, positions, ffn1_norm, ffn1_w_gate, ffn1_w_up, ffn1_w_down, mix_norm, w_in,
              mla_q_norm, mla_w_uq, mla_kv_norm, mla_w_ukv, diff_lambda_q1, diff_lambda_k1,
              diff_lambda_q2, diff_lambda_k2, diff_head_norm, w_out, ffn2_norm, ffn2_w_gate,
              ffn2_w_up, ffn2_w_down, final_norm):
    for l in range(DEPTH):
        x = x + 0.5 * swiglu(rmsnorm(x, ffn1_norm[l]), ffn1_w_gate[l], ffn1_w_up[l], ffn1_w_down[l])
        x = x + token_mixing(rmsnorm(x, mix_norm[l]), positions, w_in[l], mla_q_norm[l], mla_w_uq[l],
                             mla_kv_norm[l], mla_w_ukv[l], diff_lambda_q1[l], diff_lambda_k1[l],
                             diff_lambda_q2[l], diff_lambda_k2[l], diff_head_norm[l], w_out[l], l)
        x = x + 0.5 * swiglu(rmsnorm(x, ffn2_norm[l]), ffn2_w_gate[l], ffn2_w_up[l], ffn2_w_down[l])
    return rmsnorm(x, final_norm)
```

```python
import math
import contextlib
import numpy as np
import ml_dtypes
import concourse.bass as bass
import concourse.mybir as mybir
from concourse.bass_utils import run_bass_kernel_spmd

F32 = mybir.dt.float32
BF16 = mybir.dt.bfloat16
F16 = mybir.dt.float16
I32 = mybir.dt.int32
AF = mybir.ActivationFunctionType
ALU = mybir.AluOpType

SAME_ENGINE_SYNC = True

S = 2048
D = 1024
DFF = 2816
NIN = 2208
DEPTH = 2
EPS = 1e-6


class Res:
    __slots__ = ("name", "w", "r")

    def __init__(self, name=""):
        self.name = name
        self.w = None
        self.r = []


class Op:
    __slots__ = ("eng", "fn", "deps", "dma_sem", "dma_val", "sig", "need_sig", "id")


class Prog:
    ENGS = ("pe", "act", "dve", "pool", "sp")

    def __init__(self, nc):
        self.nc = nc
        self.ops = []
        self.last = {e: None for e in self.ENGS}
        self.dma_since_barrier = []
        self.dma_counts = {}

    def add(self, eng, fn, reads=(), writes=(), dma_sem=None):
        op = Op()
        op.id = len(self.ops)
        op.eng = eng
        op.fn = fn
        def _flat(xs):
            out_ = []
            for x_ in xs:
                if isinstance(x_, (list, tuple)):
                    out_.extend(_flat(x_))
                else:
                    out_.append(x_)
            return out_
        reads = _flat(reads)
        writes = _flat(writes)
        deps = set()
        for r in reads:
            if r.w is not None:
                deps.add(r.w)
        for w in writes:
            if w.w is not None:
                deps.add(w.w)
            deps.update(w.r)
        op.deps = deps
        op.dma_sem = dma_sem
        op.dma_val = None
        if dma_sem is not None:
            k = id(dma_sem)
            self.dma_counts[k] = self.dma_counts.get(k, 0) + 16
            op.dma_val = self.dma_counts[k]
            self.dma_since_barrier.append(op.id)
        op.sig = None
        op.need_sig = False
        self.ops.append(op)
        for r in reads:
            r.r.append(op.id)
        for w in writes:
            w.w = op.id
            w.r = []
        self.last[eng] = op.id
        return op

    def barrier(self):
        ids = [v for v in self.last.values() if v is not None] + list(self.dma_since_barrier)
        self.dma_since_barrier = []
        for e in self.ENGS:
            op = self.add(e, lambda eng: eng.nop())
            op.deps.update(i for i in ids if i != op.id)

    def emit(self, sems):
        nc = self.nc
        ops = self.ops
        for op in ops:
            for d in op.deps:
                p = ops[d]
                if p.dma_sem is not None:
                    continue
                if p.eng == op.eng and op.dma_sem is None:
                    if p.eng == "pe" or not SAME_ENGINE_SYNC:
                        continue
                p.need_sig = True
        cnt = {e: 0 for e in self.ENGS}
        for op in ops:
            if op.need_sig:
                cnt[op.eng] += 1
                op.sig = cnt[op.eng]
        per = {e: [o for o in ops if o.eng == e] for e in self.ENGS}

        def run(ename, eng):
            waited = {}
            for op in per[ename]:
                need = {}
                for d in op.deps:
                    p = ops[d]
                    if p.dma_sem is not None:
                        key = ("d", id(p.dma_sem))
                        if need.get(key, (None, 0))[1] < p.dma_val:
                            need[key] = (p.dma_sem, p.dma_val)
                    else:
                        if p.eng == ename and op.dma_sem is None:
                            if ename == "pe" or not SAME_ENGINE_SYNC:
                                continue
                        key = ("e", p.eng)
                        if need.get(key, (None, 0))[1] < p.sig:
                            need[key] = (sems[p.eng], p.sig)
                for key, (sem, val) in need.items():
                    if waited.get(key, 0) >= val:
                        continue
                    eng.wait_ge(sem, val)
                    waited[key] = val
                ins = op.fn(eng)
                if op.dma_sem is not None:
                    ins.then_inc(op.dma_sem, 16)
                elif op.need_sig:
                    ins.then_inc(sems[ename], 1)

        with nc.Block() as block:
            @block.tensor
            def _(e):
                run("pe", e)

            @block.scalar
            def _(e):
                run("act", e)

            @block.vector
            def _(e):
                run("dve", e)

            @block.gpsimd
            def _(e):
                run("pool", e)

            @block.sync
            def _(e):
                run("sp", e)


class Arena:
    def __init__(self, ap, nwords):
        self.ap = ap
        self.n = nwords
        self.top = 0
        self.peak = 0

    def alloc(self, nbytes, dtype, name=""):
        nw = (nbytes + 31) // 32 * 8
        off = self.top
        self.top += nw
        self.peak = max(self.peak, self.top)
        assert self.top <= self.n, ("SBUF arena overflow", name, self.top * 4)
        v = self.ap[:, off:off + nw]
        if dtype != F32:
            v = v.bitcast(dtype)
        return v

    def mark(self):
        return self.top

    def release(self, m):
        self.top = m


def _host_consts():
    ident = np.eye(128, dtype=np.float32)
    p = np.arange(128)[:, None]
    j = np.arange(4096)[None, :]
    dl = p - (j - 1920)
    c = (np.abs(dl) <= 64).astype(np.float32)
    c += ((dl % 4 == 0) & (np.abs(dl) <= 256)).astype(np.float32)
    c += ((dl % 16 == 0) & (np.abs(dl) <= 1024)).astype(np.float32)
    cmask = c.astype(ml_dtypes.bfloat16)
    half = 16
    inv = (np.float32(10000.0) ** (-np.arange(half, dtype=np.float32) / np.float32(half))).astype(np.float32)
    rc = np.zeros((128, 16), dtype=np.float32)
    pp = np.arange(128)
    rc[:, 0] = inv[pp % 16]
    rc[:, 1] = np.where((pp % 32) < 16, -1.0, 1.0)
    rc[:, 2] = -math.pi
    rc[:, 3] = EPS
    rc[:, 4] = 0.0
    slopes = [2.0 ** (-(2 * h + 1)) for h in range(4)] + [2.0 ** (-(2 * h + 2)) for h in range(4)]
    for j, sj in enumerate(slopes):
        for i in range(4):
            p_ = 4 * j + i
            rc[p_, 5] = sj if i == 0 else 0.0
            rc[p_, 6] = sj if i == 1 else 0.0
            rc[p_, 7] = 1.0 if i >= 2 else 0.0
            rc[p_, 8] = -sj if i == 2 else 0.0
            rc[p_, 9] = -sj if i == 3 else 0.0
            rc[p_, 10] = 1.0 if i < 2 else 0.0
    return ident, cmask, rc


PARAM_SHAPES = {
    "ffn1_norm": [DEPTH, D], "ffn1_w_gate": [DEPTH, D, DFF], "ffn1_w_up": [DEPTH, D, DFF],
    "ffn1_w_down": [DEPTH, DFF, D], "mix_norm": [DEPTH, D], "w_in": [DEPTH, D, NIN],
    "mla_q_norm": [DEPTH, 384], "mla_w_uq": [DEPTH, 384, 768], "mla_kv_norm": [DEPTH, 256],
    "mla_w_ukv": [DEPTH, 256, 1024], "diff_lambda_q1": [DEPTH, 32], "diff_lambda_k1": [DEPTH, 32],
    "diff_lambda_q2": [DEPTH, 32], "diff_lambda_k2": [DEPTH, 32], "diff_head_norm": [DEPTH, 64],
    "w_out": [DEPTH, D, D], "ffn2_norm": [DEPTH, D], "ffn2_w_gate": [DEPTH, D, DFF],
    "ffn2_w_up": [DEPTH, D, DFF], "ffn2_w_down": [DEPTH, DFF, D], "final_norm": [D],
}

ARENA_WORDS = 53200


def build(stop=None, debug=False):
    nc = bass.Bass("TRN2", target_bir_lowering=False)
    dr = {}
    dr["x"] = nc.dram_tensor("x", [S, D], F32, kind="ExternalInput").ap()
    dr["positions"] = nc.dram_tensor("positions", [S], I32, kind="ExternalInput").ap()
    for k, shp in PARAM_SHAPES.items():
        dr[k] = nc.dram_tensor(k, shp, F32, kind="ExternalInput").ap()
    dr["c_ident"] = nc.dram_tensor("c_ident", [128, 128], F32, kind="ExternalInput").ap()
    dr["c_mask"] = nc.dram_tensor("c_mask", [128, 4096], BF16, kind="ExternalInput").ap()
    dr["c_rc"] = nc.dram_tensor("c_rc", [128, 16], F32, kind="ExternalInput").ap()
    out = nc.dram_tensor("out", [S, D], F32, kind="ExternalOutput").ap()
    dbg = None
    dbgm = None
    if debug:
        dbg = nc.dram_tensor("dbg", [128, 8, S], F32, kind="ExternalOutput").ap()
        dbgm = nc.dram_tensor("dbgm", [DEPTH, 8, 128, S], BF16, kind="ExternalOutput").ap()

    ddumps = {}

    def ddump(P, name, ap, shape, dtype, reads, sem):
        if not debug or name in ddumps:
            return
        t = nc.dram_tensor(name, list(shape), dtype, kind="ExternalOutput").ap()
        ddumps[name] = t
        P.add("sp", lambda e: e.dma_start(out=t, in_=ap), reads=reads, dma_sem=sem)

    with contextlib.ExitStack() as es:
        arena_t = es.enter_context(nc.sbuf_tensor("arena", [128, ARENA_WORDS], F32))
        ps = es.enter_context(nc.psum_tensor("ps", [128, 4096], F32))
        sems = {e: es.enter_context(nc.semaphore("s_" + e)) for e in Prog.ENGS}
        dsem_pool = [es.enter_context(nc.semaphore("d%d" % i)) for i in range(64)]
        dsem_i = [0]

        def dsem():
            s_ = dsem_pool[dsem_i[0]]
            dsem_i[0] += 1
            return s_

        P = Prog(nc)
        A = Arena(arena_t, ARENA_WORDS)

        SL = [ps[:, i * 1024:(i + 1) * 1024] for i in range(4)]
        RB = [Res("bank%d" % i) for i in range(8)]
        RSL = [[RB[2 * i], RB[2 * i + 1]] for i in range(4)]
        SBK = [ps[:, i * 512:(i + 1) * 512] for i in range(4)]
        slot_i = [0]

        def next_slot():
            i = slot_i[0] % 4
            slot_i[0] += 1
            return SL[i], RSL[i]

        sslot_i = [0]
        aslot_i = [0]

        def s_bank():
            i = sslot_i[0] % 4
            sslot_i[0] += 1
            return SBK[i], RB[i]

        def acc_slot():
            i = 2 + aslot_i[0] % 2
            aslot_i[0] += 1
            return SL[i], RSL[i]

        xT = A.alloc(8 * S * 4, F32, "xT").rearrange("p (c t) -> p c t", c=8)
        RX = [[Res("x%d_%d" % (c, h)) for h in range(2)] for c in range(8)]
        cst = A.alloc(384 * 4, F32, "cst")
        Rcst = Res("cst")
        ones_bf = A.alloc(128 * 2, BF16, "ones_bf")
        ident = A.alloc(128 * 4, F32, "ident")
        Rconst = Res("const")
        sem_c = dsem()

        col = {}
        cc = [0]

        def ccol(name, n):
            col[name] = cc[0]
            cc[0] += n

        for l in range(DEPTH):
            ccol("g1_%d" % l, 8); ccol("gm_%d" % l, 8); ccol("g2_%d" % l, 8)
            ccol("gq_%d" % l, 3); ccol("gkv_%d" % l, 2); ccol("gdh_%d" % l, 1)
            ccol("lq1_%d" % l, 32); ccol("lk1_%d" % l, 32); ccol("lq2_%d" % l, 32); ccol("lk2_%d" % l, 32)
            ccol("nlam_%d" % l, 1); ccol("gc_%d" % l, 1); ccol("t1_%d" % l, 1); ccol("t2_%d" % l, 1)
        ccol("rc", 16)
        ccol("scr", 32)
        assert cc[0] <= 384, cc[0]

        def cs(name, n=1, off=0):
            return cst[:, col[name] + off:col[name] + off + n]

        cst_parts = []

        def small_dma(out_ap, in_ap):
            r_ = Res("cstp")
            cst_parts.append(r_)
            P.add("sp", lambda e, o=out_ap, i=in_ap: e.dma_start(out=o, in_=i, allow_slow_non_contiguous=True),
                  writes=[r_], dma_sem=sem_c)

        for l in range(DEPTH):
            small_dma(cs("g1_%d" % l, 8), dr["ffn1_norm"][l].rearrange("(c p) -> p c", p=128))
            small_dma(cs("gm_%d" % l, 8), dr["mix_norm"][l].rearrange("(c p) -> p c", p=128))
            small_dma(cs("g2_%d" % l, 8), dr["ffn2_norm"][l].rearrange("(c p) -> p c", p=128))
            small_dma(cs("gq_%d" % l, 3), dr["mla_q_norm"][l].rearrange("(c p) -> p c", p=128))
            small_dma(cs("gkv_%d" % l, 2), dr["mla_kv_norm"][l].rearrange("(c p) -> p c", p=128))
            g64 = dr["diff_head_norm"][l].rearrange("(p o) -> p o", o=1)
            small_dma(cst[0:64, col["gdh_%d" % l]:col["gdh_%d" % l] + 1], g64)
            small_dma(cst[64:128, col["gdh_%d" % l]:col["gdh_%d" % l] + 1], g64)
            for nm, key in (("lq1", "diff_lambda_q1"), ("lk1", "diff_lambda_k1"),
                            ("lq2", "diff_lambda_q2"), ("lk2", "diff_lambda_k2")):
                small_dma(cs("%s_%d" % (nm, l), 32), dr[key][l].partition_broadcast(128))
        small_dma(cs("rc", 16), dr["c_rc"])
        P.add("dve", lambda e: e.memset(cs("scr", 32), 0.0), reads=cst_parts, writes=[Rcst])
        sem_id = dsem()
        P.add("sp", lambda e: e.dma_start(out=ident, in_=dr["c_ident"]), writes=[Rconst], dma_sem=sem_id)
        P.add("dve", lambda e: e.memset(ones_bf, 1.0), writes=[Rconst])
        c_inv = cs("rc", 1, 0); c_sign = cs("rc", 1, 1); c_negpi = cs("rc", 1, 2); c_eps = cs("rc", 1, 3)
        c_zero = cs("rc", 1, 4)

        for l in range(DEPTH):
            lam_init = 0.8 - 0.6 * math.exp(-0.3 * l)
            scr = cs("scr", 32)
            for a_, b_, t_ in (("lq1", "lk1", "t1"), ("lq2", "lk2", "t2")):
                P.add("dve", lambda e, a_=a_, b_=b_, l=l: e.tensor_tensor(
                    scr, cs("%s_%d" % (a_, l), 32), cs("%s_%d" % (b_, l), 32), ALU.mult),
                    reads=[Rcst], writes=[Rcst])
                P.add("dve", lambda e, t_=t_, l=l: e.reduce_sum(cs("%s_%d" % (t_, l)), scr, mybir.AxisListType.X),
                      reads=[Rcst], writes=[Rcst])
                P.add("act", lambda e, t_=t_, l=l: e.activation(cs("%s_%d" % (t_, l)), cs("%s_%d" % (t_, l)), AF.Exp),
                      reads=[Rcst], writes=[Rcst])
            P.add("dve", lambda e, l=l, li=lam_init: e.scalar_tensor_tensor(
                cs("nlam_%d" % l), cs("t2_%d" % l), -li, cs("t1_%d" % l), ALU.add, ALU.subtract),
                reads=[Rcst], writes=[Rcst])
            P.add("dve", lambda e, l=l, li=lam_init: e.tensor_scalar(
                cs("gc_%d" % l), cs("gdh_%d" % l), 1.0 - li, None, ALU.mult),
                reads=[Rcst], writes=[Rcst])

        m0 = A.mark()
        xin = [A.alloc(D * 4, F32, "xin%d" % i) for i in range(2)]
        Rxin = [Res("xin%d" % i) for i in range(2)]
        sxin = [dsem(), dsem()]
        for tt in range(16):
            b = tt % 2
            P.add("sp", lambda e, tt=tt, b=b: e.dma_start(out=xin[b], in_=dr["x"][tt * 128:(tt + 1) * 128, :]),
                  writes=[Rxin[b]], dma_sem=sxin[b])
            sl, rsl = next_slot()
            for c in range(8):
                P.add("pe", lambda e, sl=sl, c=c, b=b: e.transpose(sl[:, c * 128:(c + 1) * 128],
                                                                  xin[b][:, c * 128:(c + 1) * 128], ident),
                      reads=[Rxin[b], Rconst], writes=[rsl])
            h = tt // 8
            eng = "act" if tt % 2 == 0 else "dve"
            dst = xT[:, :, tt * 128:(tt + 1) * 128]
            src = sl.rearrange("p (c t) -> p c t", c=8)
            if eng == "act":
                P.add("act", lambda e, dst=dst, src=src: e.copy(dst, src), reads=[rsl],
                      writes=[RX[c][h] for c in range(8)])
            else:
                P.add("dve", lambda e, dst=dst, src=src: e.tensor_copy(dst, src), reads=[rsl],
                      writes=[RX[c][h] for c in range(8)])
        P.barrier()
        A.release(m0)

        def rms_feature(gname, hT, RH, tmp_sq, Rsq, rs, Rrs):
            slots = {}

            def stage1(tc):
                h = tc // 2
                tsl = slice(tc * 512, (tc + 1) * 512)
                sl, rsl = next_slot()
                slots[tc] = (sl, rsl)
                for c in range(8):
                    P.add("act", lambda e, c=c, tsl=tsl: e.activation(tmp_sq[:, c, :], xT[:, c, tsl], AF.Square),
                          reads=[RX[c][h]], writes=[Rsq[c]])
                    P.add("pe", lambda e, c=c, sl=sl: e.matmul(sl[:, 0:512], ones_bf, tmp_sq[:, c, :], start=(c == 0), stop=(c == 7)),
                          reads=[Rsq[c], Rconst], writes=[rsl])

            def stage2(tc):
                h = tc // 2
                tsl = slice(tc * 512, (tc + 1) * 512)
                sl, rsl = slots[tc]
                P.add("act", lambda e, sl=sl, tsl=tsl: e.activation(rs[:, tsl], sl[:, 0:512], AF.Ln, bias=c_eps, scale=1.0 / D),
                      reads=[rsl, Rcst], writes=[Rrs[tc]])
                P.add("act", lambda e, tsl=tsl: e.activation(rs[:, tsl], rs[:, tsl], AF.Exp, scale=-0.5), reads=[Rrs[tc]], writes=[Rrs[tc]])
                for c in range(8):
                    P.add("dve", lambda e, c=c, tsl=tsl: e.scalar_tensor_tensor(
                        hT[:, c, tsl], xT[:, c, tsl], cs(gname, 1, c), rs[:, tsl], ALU.mult, ALU.mult),
                        reads=[RX[c][h], Rrs[tc], Rcst], writes=[RH[tc]])

            stage1(0)
            for tc in range(4):
                if tc + 1 < 4:
                    stage1(tc + 1)
                stage2(tc)

        def ffn(l, which):
            m = A.mark()
            hT = A.alloc(8 * S * 2, BF16, "hT").rearrange("p (c t) -> p c t", c=8)
            RH = [Res("h%d" % i) for i in range(4)]
            aT = A.alloc(12 * S * 2, BF16, "aT").rearrange("p (c t) -> p c t", c=12)
            RA = [[Res("a%d_%d" % (c, h)) for h in range(2)] for c in range(12)]
            wgu = [A.alloc(2 * 8 * 256 * 2, BF16, "wgu%d" % i).rearrange("p (g c f) -> p g c f", g=2, c=8) for i in range(2)]
            Rwgu = [Res("wgu%d" % i) for i in range(2)]
            swgu = [dsem() for _ in range(2)]
            wd = [A.alloc(12 * 128 * 2, BF16, "wd%d" % i).rearrange("p (c f) -> p c f", c=12) for i in range(3)]
            Rwd = [Res("wd%d" % i) for i in range(3)]
            swd = [dsem() for _ in range(3)]
            sg = [A.alloc(1024 * 4, F32, "sg%d" % i) for i in range(2)]
            Rsg = [Res("sg%d" % i) for i in range(2)]
            tmp_sq = A.alloc(8 * 512 * 2, BF16, "sq").rearrange("p (c t) -> p c t", c=8)
            Rsq = [Res("sq%d" % c) for c in range(8)]
            rs = A.alloc(S * 4, F32, "rs")
            Rrs = [Res("rs%d" % i) for i in range(4)]
            pre = "ffn%d_" % which
            wg_d, wu_d, wd_d = dr[pre + "w_gate"][l], dr[pre + "w_up"][l], dr[pre + "w_down"][l]
            rms_feature("g%d_%d" % (which, l), hT, RH, tmp_sq, Rsq, rs, Rrs)
            wcnt = 0
            dcnt = 0
            scnt = 0
            for (f0, nfc) in ((0, 12), (12, 10)):
                for pr in range(nfc // 2):
                    s_ = wcnt % 2
                    wcnt += 1
                    fa = (f0 + 2 * pr) * 128
                    P.add("pool", lambda e, s_=s_, fa=fa: e.dma_start(
                        out=wgu[s_][:, 0], in_=wg_d[:, fa:fa + 256].rearrange("(c p) f -> p c f", p=128)),
                        writes=[Rwgu[s_]], dma_sem=swgu[s_])
                    P.add("pool", lambda e, s_=s_, fa=fa: e.dma_start(
                        out=wgu[s_][:, 1], in_=wu_d[:, fa:fa + 256].rearrange("(c p) f -> p c f", p=128)),
                        writes=[Rwgu[s_]], dma_sem=swgu[s_])
                    for j in range(2):
                        fcl = 2 * pr + j
                        for th in range(2):
                            gsl, rg = next_slot()
                            usl, ru = next_slot()
                            for (g_, sl_, r_) in ((0, gsl, rg), (1, usl, ru)):
                                for c in range(8):
                                    for n in range(2):
                                        tq = th * 2 + n
                                        P.add("pe", lambda e, g_=g_, sl_=sl_, c=c, n=n, s_=s_, j=j, tq=tq: e.matmul(
                                            sl_[:, n * 512:(n + 1) * 512], wgu[s_][:, g_, c, j * 128:(j + 1) * 128],
                                            hT[:, c, tq * 512:(tq + 1) * 512], start=(c == 0), stop=(c == 7)),
                                            reads=[Rwgu[s_], RH[tq]], writes=[r_])
                            b = scnt % 2
                            scnt += 1
                            P.add("act", lambda e, b=b, gsl=gsl: e.activation(sg[b], gsl, AF.Silu),
                                  reads=[rg], writes=[Rsg[b]])
                            P.add("dve", lambda e, b=b, usl=usl, fcl=fcl, th=th: e.tensor_tensor(
                                aT[:, fcl, th * 1024:(th + 1) * 1024], sg[b], usl, ALU.mult),
                                reads=[Rsg[b], ru], writes=[RA[fcl][th]])
                for dc in range(8):
                    s_ = dcnt % 3
                    dcnt += 1
                    P.add("pool", lambda e, s_=s_, dc=dc, f0=f0, nfc=nfc: e.dma_start(
                        out=wd[s_][:, 0:nfc, :],
                        in_=wd_d[f0 * 128:(f0 + nfc) * 128, dc * 128:(dc + 1) * 128].rearrange("(c p) f -> p c f", p=128)),
                        writes=[Rwd[s_]], dma_sem=swd[s_])
                    for th in range(2):
                        sl, rsl = next_slot()
                        for fcl in range(nfc):
                            for n in range(2):
                                P.add("pe", lambda e, sl=sl, fcl=fcl, n=n, s_=s_, th=th, nfc=nfc: e.matmul(
                                    sl[:, n * 512:(n + 1) * 512], wd[s_][:, fcl, :],
                                    aT[:, fcl, th * 1024 + n * 512:th * 1024 + (n + 1) * 512],
                                    start=(fcl == 0), stop=(fcl == nfc - 1)),
                                    reads=[Rwd[s_], RA[fcl][th]], writes=[rsl])
                        xs = xT[:, dc, th * 1024:(th + 1) * 1024]
                        P.add("dve", lambda e, sl=sl, xs=xs: e.scalar_tensor_tensor(xs, sl, 0.5, xs, ALU.mult, ALU.add),
                              reads=[rsl, RX[dc][th]], writes=[RX[dc][th]])
            P.barrier()
            A.release(m)

        def mixer(l):
            m = A.mark()
            RH = [Res("h%d" % i) for i in range(4)]
            cqn_flat = A.alloc(3 * S * 2, BF16, "cqn")
            cqn = cqn_flat.rearrange("p (c t) -> p c t", c=3)
            Rcqn = [Res("cqn%d" % h) for h in range(2)]
            ckvn_flat = A.alloc(2 * S * 2, BF16, "ckvn")
            ckvn = ckvn_flat.rearrange("p (c t) -> p c t", c=2)
            Rckvn = [Res("ckvn%d" % h) for h in range(2)]
            krope = A.alloc(S * 2, BF16, "krope")
            Rkrope = Res("krope")
            w_in = dr["w_in"][l]
            pairbuf = A.alloc(8192 * 2 + 16 * 192 * 2, BF16, "pair")
            Rq = [Res("q%d" % i) for i in range(2)]
            Rk = [Res("k%d" % i) for i in range(2)]
            Rv = Res("v")
            vaug = pairbuf[:, 8192:8192 + 16 * 192].rearrange("p (k e) -> p k e", k=16)
            mixedall = A.alloc(2 * S * 2, BF16, "mixedall")
            mixed = [mixedall[:, i * S:(i + 1) * S] for i in range(2)]
            Rmixed = [Res("mixed%d" % i) for i in range(2)]
            wout = [A.alloc(D * 2, BF16, "wout%d" % i) for i in range(2)]
            Rwout = [Res("wout%d" % i) for i in range(2)]
            swout = [dsem(), dsem()]
            NPT = 5
            PTall = A.alloc(NPT * 512 * 2, BF16, "PT")
            RPT6 = [Res("PT%d" % i) for i in range(NPT)]
            PT6 = [PTall[:, i * 512:(i + 1) * 512] for i in range(NPT)]
            bcrr = A.alloc(2048 * 4, F32, "bcrr")
            bcs = bcrr[:, 0:1024]
            Rbcs = Res("bcs")
            rrow = bcrr[:, 1024:2048]
            Rrrow = Res("rrow")
            Rscr = Res("scr")
            ropeA, RropeA = bcs, Rbcs
            ropeB, RropeB = rrow, Rrrow
            wst = [A.alloc(8 * 384 * 2, BF16, "wst%d" % i) for i in range(2)]
            Rwst = [Res("wst%d" % i) for i in range(2)]
            swst = [dsem(), dsem()]
            cnt = {"pt": 0, "wst": 0, "mix": 0, "tmp": 0, "dist": 0, "ptm": 0}
            hT = A.alloc(8 * S * 2, BF16, "hT").rearrange("p (c t) -> p c t", c=8)
            m1 = A.mark()
            tmp_sq = A.alloc(8 * 512 * 2, BF16, "sq").rearrange("p (c t) -> p c t", c=8)
            Rsq = [Res("sq%d" % c) for c in range(8)]
            rs = A.alloc(S * 4, F32, "rs")
            Rrs = [Res("rs%d" % i) for i in range(4)]
            rms_feature("gm_%d" % l, hT, RH, tmp_sq, Rsq, rs, Rrs)
            P.barrier()
            A.release(m1)


            def load_w(cols_list):
                b = cnt["wst"] % 2
                cnt["wst"] += 1
                tot = sum(n for _, n in cols_list)
                v = wst[b][:, 0:8 * tot].rearrange("p (c f) -> p c f", c=8)
                o = 0
                for (c0, n) in cols_list:
                    P.add("pool", lambda e, v=v, o=o, c0=c0, n=n: e.dma_start(
                        out=v[:, :, o:o + n], in_=w_in[:, c0:c0 + n].rearrange("(c p) f -> p c f", p=128)),
                        writes=[Rwst[b]], dma_sem=swst[b])
                    o += n
                return v, Rwst[b]

            def proj_fm(wv, rw, wc0, M, dst_fn, dst_res_fn, scale, src=None, rsrc=None, nck=8):
                src = hT if src is None else src
                for th in range(2):
                    sl, rsl = next_slot()
                    for c in range(nck):
                        for n in range(2):
                            tq = th * 2 + n
                            rr = RH[tq] if rsrc is None else rsrc[th]
                            P.add("pe", lambda e, sl=sl, c=c, n=n, tq=tq: e.matmul(
                                sl[0:M, n * 512:(n + 1) * 512], wv[:, c, wc0:wc0 + M],
                                src[:, c, tq * 512:(tq + 1) * 512], start=(c == 0), stop=(c == nck - 1)),
                                reads=[rw, rr], writes=[rsl])
                    d_ = dst_fn(th)
                    P.add("act", lambda e, d_=d_, sl=sl: e.activation(d_, sl[0:M, :], AF.Copy, scale=scale),
                          reads=[rsl], writes=[dst_res_fn(th)])

            def proj_psum(wv, rw, wc0, M, th, src=None, rsrc=None, nck=8):
                src = hT if src is None else src
                sl, rsl = next_slot()
                for c in range(nck):
                    for n in range(2):
                        tq = th * 2 + n
                        rr = RH[tq] if rsrc is None else rsrc[th]
                        P.add("pe", lambda e, sl=sl, c=c, n=n, tq=tq: e.matmul(
                            sl[0:M, n * 512:(n + 1) * 512], wv[:, c, wc0:wc0 + M],
                            src[:, c, tq * 512:(tq + 1) * 512], start=(c == 0), stop=(c == nck - 1)),
                            reads=[rw, rr], writes=[rsl])
                return sl, rsl

            def proj_v(wv, rw, cols, src=None, rsrc=None, nck=8):
                src = hT if src is None else src
                P.add("dve", lambda e: e.memset(vaug[:, :, 64:128], 1.0), writes=[Rv])
                for k4 in range(4):
                    sl, rsl = next_slot()
                    for kk in range(4):
                        kt = k4 * 4 + kk
                        for hh in range(2):
                            for c in range(nck):
                                rr = RH[kt // 4] if rsrc is None else rsrc[kt // 8]
                                P.add("pe", lambda e, sl=sl, kk=kk, hh=hh, c=c, kt=kt: e.matmul(
                                    sl[:, kk * 128 + hh * 64:kk * 128 + hh * 64 + 64],
                                    src[:, c, kt * 128:(kt + 1) * 128], wv[:, c, cols[hh]:cols[hh] + 64],
                                    start=(c == 0), stop=(c == nck - 1)),
                                    reads=[rw, rr], writes=[rsl])
                    for hv in range(2):
                        P.add("act", lambda e, sl=sl, k4=k4, hv=hv: e.copy(
                            vaug[:, k4 * 4:(k4 + 1) * 4, hv * 128:hv * 128 + 64],
                            sl[:, 0:512].rearrange("p (k h e) -> p k h e", k=4, h=2)[:, :, hv, :]),
                            reads=[rsl], writes=[Rv])

            sdbgm = dsem() if dbgm is not None else None

            wq = []

            def wout_later(chunk, mb):
                flush_pv()
                wq.append((chunk, mb))

            def wout_drain():
                while len(wq) >= 2:
                    wout_apply2(wq.pop(0), wq.pop(0))

            def wout_apply2(a_, b_):
                flush_pv()
                items = (a_, b_)
                for (chunk, mb) in items:
                    b = chunk % 2
                    if dbgm is not None:
                        P.add("sp", lambda e, chunk=chunk, mb=mb: e.dma_start(out=dbgm[l, chunk], in_=mixed[mb]),
                              reads=[Rmixed[mb]], dma_sem=sdbgm)
                    P.add("pool", lambda e, b=b, chunk=chunk: e.dma_start(
                        out=wout[b], in_=dr["w_out"][l][chunk * 128:(chunk + 1) * 128, :]),
                        writes=[Rwout[b]], dma_sem=swout[b])
                for dc in range(8):
                    for th in range(2):
                        sl, rsl = next_slot()
                        for j, (chunk, mb) in enumerate(items):
                            b = chunk % 2
                            for n in range(2):
                                P.add("pe", lambda e, sl=sl, n=n, dc=dc, th=th, b=b, mb=mb, j=j: e.matmul(
                                    sl[:, n * 512:(n + 1) * 512], wout[b][:, dc * 128:(dc + 1) * 128],
                                    mixed[mb][:, th * 1024 + n * 512:th * 1024 + (n + 1) * 512],
                                    start=(j == 0), stop=(j == 1)),
                                    reads=[Rwout[b], Rmixed[mb]], writes=[rsl])
                        xs = xT[:, dc, th * 1024:(th + 1) * 1024]
                        P.add("dve", lambda e, sl=sl, xs=xs: e.tensor_tensor(xs, sl, xs, ALU.add),
                              reads=[rsl, RX[dc][th]], writes=[RX[dc][th]])

            def wout_apply(chunk, mb):
                flush_pv()
                b = chunk % 2
                if dbgm is not None:
                    P.add("sp", lambda e, chunk=chunk, mb=mb: e.dma_start(out=dbgm[l, chunk], in_=mixed[mb]),
                          reads=[Rmixed[mb]], dma_sem=sdbgm)
                P.add("pool", lambda e, b=b, chunk=chunk: e.dma_start(
                    out=wout[b], in_=dr["w_out"][l][chunk * 128:(chunk + 1) * 128, :]),
                    writes=[Rwout[b]], dma_sem=swout[b])
                for dc in range(8):
                    for th in range(2):
                        sl, rsl = next_slot()
                        for n in range(2):
                            P.add("pe", lambda e, sl=sl, n=n, dc=dc, th=th, b=b: e.matmul(
                                sl[:, n * 512:(n + 1) * 512], wout[b][:, dc * 128:(dc + 1) * 128],
                                mixed[mb][:, th * 1024 + n * 512:th * 1024 + (n + 1) * 512], start=True, stop=True),
                                reads=[Rwout[b], Rmixed[mb]], writes=[rsl])
                        xs = xT[:, dc, th * 1024:(th + 1) * 1024]
                        P.add("dve", lambda e, sl=sl, xs=xs: e.tensor_tensor(xs, sl, xs, ALU.add),
                              reads=[rsl, RX[dc][th]], writes=[RX[dc][th]])

            pend = []
            LA = 3

            pexp = []
            EXPD = 1

            def flush_pv(keep=0):
                if keep == 0:
                    while pexp:
                        pexp.pop(0)()
                while pend:
                    ntile_after = sum(1 for j in pend[1:] if j.get("tile"))
                    if ntile_after < keep:
                        break
                    job = pend.pop(0)
                    if job["mask"] is not None:
                        job["mask"]()
                    job["fn"]()

            def attend(kT, rk, qT, rq, hh, bias, fin, qhs=(0, 1)):
                if bias is not None:
                    P.add("dve", lambda e: e.tensor_scalar(bias["posqh"], bias["posq"], 2.0 * float(bias["slope"]), None, ALU.mult),
                          reads=[bias["rpos"]], writes=[bias["rposh"]])
                    P.add("dve", lambda e: e.tensor_scalar(bias["poskh"], bias["posk"], 2.0 * float(bias["slope"]), None, ALU.mult),
                          reads=[bias["rpos"]], writes=[bias["rposh"]])
                for qh in qhs:
                    acc, racc = acc_slot()
                    act_fn = (bias or {}).get("active")
                    tiles = [(kt, n) for kt in range(16) for n in range(2) if act_fn is None or act_fn(qh, n, kt)]
                    first_kt = {n: min(kt for kt, n_ in tiles if n_ == n) for n in range(2)}
                    last_kt = {n: max(kt for kt, n_ in tiles if n_ == n) for n in range(2)}
                    rready = {}

                    def emit_r(kt_, n_):
                        db_ = cnt["dist"] % 4
                        cnt["dist"] += 1
                        q0_ = qh * 1024 + n_ * 512
                        P.add("dve", lambda e: e.tensor_scalar(
                            bias["dist"][db_], bias["posqh"][:, q0_:q0_ + 512], bias["poskh"][:, kt_:kt_ + 1],
                            0.0, ALU.subtract, ALU.max),
                            reads=[bias["rposh"]], writes=[bias["rdist"][db_]])
                        rready[(kt_, n_)] = db_

                    for ti, (kt, n) in enumerate(tiles):
                        if True:
                            q0 = qh * 1024 + n * 512
                            if pend and pend[0]["mask"] is not None and sum(1 for j in pend[1:] if j.get("tile")) >= LA - 1:
                                pend[0]["mask"]()
                                pend[0]["mask"] = None
                            sb, rsb = s_bank()
                            P.add("pe", lambda e, sb=sb, kt=kt, q0=q0: e.matmul(
                                sb, kT[:, kt * 128:(kt + 1) * 128], qT[:, q0:q0 + 512], start=True, stop=True),
                                reads=[rk, rq], writes=[rsb])
                            pb = cnt["pt"] % NPT
                            cnt["pt"] += 1
                            mask_job = None
                            if bias is None:
                                P.add("act", lambda e, pb=pb, sb=sb: e.activation(PT6[pb], sb, AF.Exp),
                                      reads=[rsb], writes=[RPT6[pb]])
                                pv_src, pv_res = PT6[pb], RPT6[pb]
                            else:
                                if (kt, n) not in rready:
                                    emit_r(kt, n)
                                db = rready[(kt, n)]
                                tb = cnt["tmp"] % 4
                                cnt["tmp"] += 1
                                via_act = False
                                if via_act:
                                    P.add("act", lambda e, tb=tb, sb=sb: e.copy(bias["tmp"][tb], sb),
                                          reads=[rsb], writes=[bias["rtmp"][tb]])
                                while len(pexp) >= EXPD:
                                    pexp.pop(0)()
                                if ti + 1 < len(tiles):
                                    emit_r(*tiles[ti + 1])
                                if via_act:
                                    P.add("dve", lambda e, db=db, tb=tb: e.tensor_tensor(
                                        bias["tmp"][tb], bias["tmp"][tb], bias["dist"][db], ALU.subtract),
                                        reads=[bias["rdist"][db], bias["rtmp"][tb]], writes=[bias["rtmp"][tb]])
                                else:
                                    P.add("dve", lambda e, db=db, tb=tb, sb=sb: e.tensor_tensor(
                                        bias["tmp"][tb], sb, bias["dist"][db], ALU.subtract),
                                        reads=[bias["rdist"][db], rsb], writes=[bias["rtmp"][tb]])

                                def exp_job(pb=pb, tb=tb):
                                    P.add("act", lambda e: e.activation(PT6[pb], bias["tmp"][tb], AF.Exp),
                                          reads=[bias["rtmp"][tb]], writes=[RPT6[pb]])
                                pexp.append(exp_job)
                                pv_src, pv_res = PT6[pb], RPT6[pb]
                                if bias.get("mask") is not None:
                                    mb_ = cnt["ptm"] % 4
                                    cnt["ptm"] += 1
                                    j0 = q0 - 128 * kt + 1920

                                    def mask_job(mb_=mb_, pb=pb, j0=j0):
                                        P.add("dve", lambda e: e.tensor_tensor(
                                            bias["ptm"][mb_], PT6[pb], bias["mask"][:, j0:j0 + 512], ALU.mult),
                                            reads=[RPT6[pb], bias["rmask"]], writes=[bias["rptm"][mb_]])
                                    pv_src, pv_res = bias["ptm"][mb_], bias["rptm"][mb_]

                            def pv_job(acc=acc, racc=racc, kt=kt, n=n, qh=qh, pv_src=pv_src, pv_res=pv_res, first_kt=first_kt, last_kt=last_kt, tiles=tiles):
                                P.add("pe", lambda e: e.matmul(
                                    acc[:, n * 512:(n + 1) * 512], vaug[:, kt, hh * 64:hh * 64 + 128], pv_src,
                                    start=(kt == first_kt[n]), stop=(kt == last_kt[n])),
                                    reads=[Rv, pv_res], writes=[racc])
                                if (kt, n) == tiles[-1]:
                                    fin(qh, acc, racc)
                            pend.append({"fn": pv_job, "mask": mask_job, "tile": True})
                            flush_pv(keep=LA)

            def recip_rows(acc, racc, hh, then, on_dve=False):
                nr = slice(hh * 64, hh * 64 + 64)
                dr_ = slice((1 - hh) * 64, (1 - hh) * 64 + 64)

                def job():
                    if on_dve:
                        P.add("dve", lambda e: e.tensor_copy(bcs[nr, :], acc[dr_, :]), reads=[racc], writes=[Rbcs])
                        P.add("dve", lambda e: e.reciprocal(bcs[nr, :], bcs[nr, :]), reads=[Rbcs], writes=[Rbcs])
                    else:
                        P.add("act", lambda e: e.activation(bcs[nr, :], acc[dr_, :], AF.Ln), reads=[racc], writes=[Rbcs])
                        P.add("act", lambda e: e.activation(bcs[nr, :], bcs[nr, :], AF.Exp, scale=-1.0), reads=[Rbcs], writes=[Rbcs])
                    then()
                pend.append({"fn": job, "mask": None})

            def simple_fin(mb, hh, on_dve=False):
                nr = slice(hh * 64, hh * 64 + 64)

                def fin(qh, acc, racc):
                    def then():
                        P.add("dve", lambda e: e.tensor_tensor(
                            mixed[mb][nr, qh * 1024:(qh + 1) * 1024], acc[nr, :], bcs[nr, :], ALU.mult),
                            reads=[racc, Rbcs], writes=[Rmixed[mb]])
                    recip_rows(acc, racc, hh, then, on_dve=on_dve)
                return fin

            mB = A.mark()
            cosT = A.alloc(S * 4, F32, "cos")
            sinT = A.alloc(S * 4, F32, "sin")
            Rtab = Res("tab")
            wqc = A.alloc(3 * 8 * 128 * 2, BF16, "wqc").rearrange("p (c h e) -> p c h e", c=3, h=8)
            wukv = A.alloc(2 * 1024 * 2, BF16, "wukv").rearrange("p (c f) -> p c f", c=2)
            swb = dsem()
            pre_cq = load_w([(768, 384)])
            pre_ckv = load_w([(1152, 256)])
            uq4 = dr["mla_w_uq"][l].rearrange("(c p) (h e) -> p c h e", p=128, e=96)
            Rwb = []

            def wb_dma(out_ap, in_ap):
                r_ = Res("wb%d" % len(Rwb))
                Rwb.append(r_)
                P.add("pool", lambda e: e.dma_start(out=out_ap, in_=in_ap, allow_slow_non_contiguous=True),
                      writes=[r_], dma_sem=swb)

            wb_dma(wukv, dr["mla_w_ukv"][l].rearrange("(c p) f -> p c f", p=128))
            for c in range(3):
                wb_dma(wqc[:, c, :, 0:96], uq4[:, c, :, 0:96])
                wb_dma(wqc[:, c, :, 96:112], uq4[:, c, :, 80:96])
                wb_dma(wqc[:, c, :, 112:128], uq4[:, c, :, 64:80])
            m2 = A.mark()
            posi = pairbuf.bitcast(I32)[:, 0:S]
            posf = mixedall.bitcast(F32)[:, 0:S]
            Rpos = Res("pos")
            spos = dsem()
            P.add("sp", lambda e: e.dma_start(out=posi, in_=dr["positions"].partition_broadcast(128)),
                  writes=[Rpos], dma_sem=spos)
            P.add("dve", lambda e: e.tensor_copy(posf, posi), reads=[Rpos], writes=[Rpos])
            P.add("dve", lambda e: e.tensor_scalar(posf, posf, c_inv, None, ALU.mult), reads=[Rpos, Rcst], writes=[Rpos])
            TWO_PI = 2.0 * math.pi
            for (tab, shift) in ((cosT, 0.5 * math.pi), (sinT, 0.0)):
                P.add("dve", lambda e, tab=tab, shift=shift: e.tensor_scalar(tab, posf, shift, None, ALU.add),
                      reads=[Rpos], writes=[Rtab])
                P.add("dve", lambda e, tab=tab: e.tensor_scalar(posi, tab, 1.0 / TWO_PI, None, ALU.mult),
                      reads=[Rtab], writes=[Rscr])
                P.add("dve", lambda e: e.tensor_copy(bcrr, posi), reads=[Rscr], writes=[Rbcs, Rrrow])
                P.add("dve", lambda e, tab=tab: e.scalar_tensor_tensor(tab, bcrr, -TWO_PI, tab, ALU.mult, ALU.add),
                      reads=[Rbcs, Rrrow, Rtab], writes=[Rtab])
                P.add("dve", lambda e, tab=tab: e.tensor_scalar(bcrr, tab, math.pi, -TWO_PI, ALU.is_gt, ALU.mult),
                      reads=[Rtab], writes=[Rbcs, Rrrow])
                P.add("dve", lambda e, tab=tab: e.tensor_tensor(tab, tab, bcrr, ALU.add),
                      reads=[Rbcs, Rrrow, Rtab], writes=[Rtab])
                P.add("dve", lambda e, tab=tab: e.tensor_scalar(tab, tab, -math.pi, math.pi, ALU.max, ALU.min),
                      reads=[Rtab], writes=[Rtab])
                P.add("act", lambda e, tab=tab: e.activation(tab, tab, AF.Sin), reads=[Rtab], writes=[Rtab])
            P.add("dve", lambda e: e.tensor_scalar(sinT, sinT, c_sign, None, ALU.mult), reads=[Rtab, Rcst], writes=[Rtab])
            if debug and l == 0:
                sdd = dsem()
                ddump(P, "d_cos", cosT, [128, S], F32, [Rtab], sdd)
                ddump(P, "d_sin", sinT, [128, S], F32, [Rtab], sdd)
            P.barrier()
            A.release(m2)
            def latent_norm(pre, nchunk, gname, dst, Rdst, nfeat):
                wv, rw = pre
                sqb = [pairbuf[:, i * 1024:(i + 1) * 1024] for i in range(3)]
                RPT = [Rq[0], Rq[1], Rk[0]]
                for th in range(2):
                    slots = [next_slot() for _ in range(nchunk)]
                    for ci, (sl, rsl) in enumerate(slots):
                        for c in range(8):
                            for n in range(2):
                                tq = th * 2 + n
                                P.add("pe", lambda e, sl=sl, c=c, n=n, tq=tq, ci=ci: e.matmul(
                                    sl[:, n * 512:(n + 1) * 512], wv[:, c, ci * 128:(ci + 1) * 128],
                                    hT[:, c, tq * 512:(tq + 1) * 512], start=(c == 0), stop=(c == 7)),
                                    reads=[rw, RH[tq]], writes=[rsl])
                    ssl, rssl = next_slot()
                    for ci, (sl, rsl) in enumerate(slots):
                        P.add("act", lambda e, sl=sl, ci=ci: e.activation(sqb[ci], sl, AF.Square), reads=[rsl], writes=[RPT[ci]])
                        for n in range(2):
                            P.add("pe", lambda e, ssl=ssl, n=n, ci=ci: e.matmul(
                                ssl[:, n * 512:(n + 1) * 512], ones_bf, sqb[ci][:, n * 512:(n + 1) * 512],
                                start=(ci == 0), stop=(ci == nchunk - 1)), reads=[RPT[ci], Rconst], writes=[rssl])
                    P.add("act", lambda e, ssl=ssl: e.activation(bcs, ssl, AF.Ln, bias=c_eps, scale=1.0 / nfeat),
                          reads=[rssl, Rcst], writes=[Rbcs])
                    P.add("act", lambda e: e.activation(bcs, bcs, AF.Exp, scale=-0.5), reads=[Rbcs], writes=[Rbcs])
                    for ci, (sl, rsl) in enumerate(slots):
                        P.add("dve", lambda e, sl=sl, ci=ci, th=th: e.scalar_tensor_tensor(
                            dst[:, ci, th * 1024:(th + 1) * 1024], sl, cs(gname, 1, ci), bcs, ALU.mult, ALU.mult),
                            reads=[rsl, Rbcs, Rcst], writes=[Rdst[th]])

            latent_norm(pre_cq, 3, "gq_%d" % l, cqn, Rcqn, 384.0)
            latent_norm(pre_ckv, 2, "gkv_%d" % l, ckvn, Rckvn, 256.0)
            wv, rw = load_w([(1408, 32), (1424, 16), (1408, 16)])
            if debug and l == 0:
                ddump(P, "d_wkpe", wv, [128, 8, 64], BF16, [rw], sdd)
            for th in range(2):
                slA, rA = next_slot()
                slB, rB = next_slot()
                for (sl_, r_, c0_) in ((slA, rA, 0), (slB, rB, 32)):
                    for c in range(8):
                        for n in range(2):
                            tq = th * 2 + n
                            op_ = P.add("pe", lambda e, sl_=sl_, c=c, n=n, tq=tq, c0_=c0_, wv=wv: e.matmul(
                                sl_[0:32, n * 512:(n + 1) * 512], wv[:, c, c0_:c0_ + 32],
                                hT[:, c, tq * 512:(tq + 1) * 512], start=(c == 0), stop=(c == 7)),
                                reads=[rw, RH[tq]], writes=[r_])
                            if not hasattr(P, "kpe_first"):
                                P.kpe_first = op_.id
                tsl = slice(th * 1024, (th + 1) * 1024)
                if debug and l == 0 and th == 0:
                    dd_ = mixedall.bitcast(F32)[:, 0:1024]
                    Rdd = Res("dd")
                    P.add("act", lambda e, slA=slA: e.copy(dd_, slA), reads=[rA], writes=[Rdd])
                    ddump(P, "d_slA", dd_, [128, 1024], F32, [Rdd], sdd)
                P.add("dve", lambda e, slA=slA, tsl=tsl: e.scalar_tensor_tensor(
                    ropeA[64:96, :], slA[0:32, :], 1.0, cosT[0:32, tsl], ALU.mult, ALU.mult),
                    reads=[rA, Rtab], writes=[RropeA])
                P.add("dve", lambda e, slB=slB, tsl=tsl: e.scalar_tensor_tensor(
                    ropeB[64:96, :], slB[0:32, :], 1.0, sinT[0:32, tsl], ALU.mult, ALU.mult),
                    reads=[rB, Rtab], writes=[RropeB])
                P.add("dve", lambda e, tsl=tsl: e.tensor_tensor(krope[64:96, tsl], ropeA[64:96, :], ropeB[64:96, :], ALU.add),
                      reads=[RropeA, RropeB], writes=[Rkrope])
                P.add("dve", lambda e, tsl=tsl: e.tensor_tensor(krope[96:128, tsl], ropeA[64:96, :], ropeB[64:96, :], ALU.add),
                      reads=[RropeA, RropeB], writes=[Rkrope])
                if debug and l == 0 and th == 0:
                    ddump(P, "d_ropeA", ropeA, [128, 1024], F32, [RropeA], sdd)
                    ddump(P, "d_ropeB", ropeB, [128, 1024], F32, [RropeB], sdd)

            qh_t = [pairbuf[:, i * 2048:(i + 1) * 2048] for i in range(2)]
            kh_t = [pairbuf[:, 4096 + i * 2048:4096 + (i + 1) * 2048] for i in range(2)]
            sc_b = 96.0 ** -0.5
            for pr in range(4):
                mb = cnt["mix"] % 2
                cnt["mix"] += 1
                for hh in range(2):
                    h = 2 * pr + hh
                    for th in range(2):
                        slA, rA = next_slot()
                        for c in range(3):
                            for n in range(2):
                                tq = th * 2 + n
                                P.add("pe", lambda e, slA=slA, c=c, n=n, tq=tq, h=h: e.matmul(
                                    slA[:, n * 512:(n + 1) * 512], wqc[:, c, h, :],
                                    cqn[:, c, tq * 512:(tq + 1) * 512], start=(c == 0), stop=(c == 2)),
                                    reads=[Rwb, Rcqn[th]], writes=[rA])
                        tsl = slice(th * 1024, (th + 1) * 1024)
                        P.add("act", lambda e, slA=slA, hh=hh, tsl=tsl: e.activation(qh_t[hh][0:64, tsl], slA[0:64, :], AF.Copy, scale=sc_b),
                              reads=[rA], writes=[Rq[hh]])
                        P.add("dve", lambda e, slA=slA, tsl=tsl, hh=hh: e.scalar_tensor_tensor(
                            qh_t[hh][64:96, tsl], slA[64:96, :], sc_b, cosT[64:96, tsl], ALU.mult, ALU.mult),
                            reads=[rA, Rtab], writes=[Rq[hh]])
                        P.add("dve", lambda e, slA=slA, tsl=tsl, hh=hh: e.scalar_tensor_tensor(
                            qh_t[hh][96:128, tsl], slA[96:128, :], sc_b, sinT[96:128, tsl], ALU.mult, ALU.mult),
                            reads=[rA, Rtab], writes=[Rq[hh]])
                    proj_fm(wukv, Rwb, h * 128, 64, lambda th, hh=hh: kh_t[hh][0:64, th * 1024:(th + 1) * 1024],
                            lambda th, hh=hh: Rk[hh], 1.0, src=ckvn, rsrc=Rckvn, nck=2)
                    P.add("dve", lambda e, hh=hh: e.tensor_copy(kh_t[hh][64:128, :], krope[64:128, :]),
                          reads=[Rkrope], writes=[Rk[hh]])
                proj_v(wukv, Rwb, [(2 * pr) * 128 + 64, (2 * pr + 1) * 128 + 64], src=ckvn, rsrc=Rckvn, nck=2)
                wout_drain()
                if debug and l == 0 and pr == 0:
                    ddump(P, "d_qk", pairbuf[:, 0:8192], [128, 8192], BF16, [Rq[0], Rq[1], Rk[0], Rk[1]], sdd)
                    ddump(P, "d_v", pairbuf[:, 8192:8192 + 2112], [128, 2112], BF16, [Rv], sdd)
                    ddump(P, "d_cqn", cqn_flat, [128, 3 * S], BF16, Rcqn, sdd)
                for hh in range(2):
                    attend(kh_t[hh][0:128, :], Rk[hh], qh_t[hh][0:128, :], Rq[hh], hh, None, simple_fin(mb, hh, on_dve=True))
                wout_later(2 + pr, mb)
            CB = 1440
            wlist = [[(pr * 128, 128), (256 + pr * 128, 128), (512 + pr * 128, 128)] for pr in range(2)] + \
                    [[(CB + pr * 128, 128), (CB + 256 + pr * 128, 128), (CB + 512 + pr * 128, 128)] for pr in range(2)]
            wnext = [load_w(wlist[0])]
            wout_drain()
            P.barrier()
            A.release(mB)

            posq = A.alloc(S * 2, F16, "posq")
            posk = A.alloc(16 * 4, F32, "posk")
            augq = A.alloc(S * 2, BF16, "augq")
            augk = A.alloc(S * 2, BF16, "augk")
            Rpos = Res("pos2")
            Raug = Res("aug")
            saug_q = [dsem(), dsem()]
            saug_k = [dsem(), dsem()]
            m3 = A.mark()
            posi = pairbuf.bitcast(I32)[:, 0:S]
            posf = mixedall.bitcast(F32)[:, 0:S]
            phi = bcrr
            plo = cqn_flat.bitcast(F32)[:, 0:S]
            poski = A.alloc(16 * 4, I32, "poski")
            spos = dsem()
            Rt = Res("augtmp")
            P.add("sp", lambda e: e.dma_start(out=posi, in_=dr["positions"].partition_broadcast(128)),
                  writes=[Rpos], dma_sem=spos)
            P.add("sp", lambda e: e.dma_start(out=poski, in_=dr["positions"].rearrange("(t p) -> p t", p=128),
                                              allow_slow_non_contiguous=True), writes=[Rpos], dma_sem=spos)
            P.add("dve", lambda e: e.tensor_copy(posq, posi), reads=[Rpos], writes=[Rpos])
            P.add("dve", lambda e: e.tensor_copy(posk, poski), reads=[Rpos], writes=[Rpos])
            P.add("dve", lambda e: e.tensor_copy(posf, posi), reads=[Rpos], writes=[Rt])
            P.add("dve", lambda e: e.tensor_scalar(posi, posf, 1.0 / 64.0, None, ALU.mult), reads=[Rt], writes=[Rpos])
            P.add("dve", lambda e: e.tensor_copy(phi, posi), reads=[Rpos], writes=[Rt])
            P.add("dve", lambda e: e.tensor_scalar(phi, phi, 64.0, None, ALU.mult), reads=[Rt], writes=[Rt])
            P.add("dve", lambda e: e.tensor_tensor(plo, posf, phi, ALU.subtract), reads=[Rt], writes=[Rt])
            rcq = [cst[0:32, col["rc"] + i:col["rc"] + i + 1] for i in range(5, 11)]
            for (dst_, ca, cb, cc_) in ((augq, rcq[0], rcq[1], rcq[2]), (augk, rcq[3], rcq[4], rcq[5])):
                P.add("dve", lambda e, cb=cb, cc_=cc_: e.tensor_scalar(posf[0:32, :], plo[0:32, :], cb, cc_, ALU.mult, ALU.add),
                      reads=[Rt, Rcst], writes=[Rt])
                P.add("dve", lambda e, dst_=dst_, ca=ca: e.scalar_tensor_tensor(
                    dst_[0:32, :], phi[0:32, :], ca, posf[0:32, :], ALU.mult, ALU.add),
                    reads=[Rt, Rcst], writes=[Raug])
            P.barrier()
            A.release(m3)
            bias = {"posq": posq, "posk": posk, "rpos": Rpos}
            bias["posqh"] = A.alloc(S * 2, F16, "posqh")
            bias["poskh"] = A.alloc(16 * 4, F32, "poskh")
            bias["rposh"] = Res("posh")
            kr16 = krope.bitcast(F16)
            bias["dist"] = [kr16[:, i * 512:(i + 1) * 512] for i in range(4)]
            bias["rdist"] = [Res("dist%d" % i) for i in range(4)]
            cq32 = cqn_flat.bitcast(F32)
            bias["tmp"] = [cq32[:, i * 512:(i + 1) * 512] for i in range(4)]
            bias["rtmp"] = [Res("tmp%d" % i) for i in range(4)]
            ptm_all = A.alloc(2048 * 2, BF16, "ptm")
            bias["ptm"] = [ptm_all[:, i * 512:(i + 1) * 512] for i in range(4)]
            bias["rptm"] = [Res("ptm%d" % i) for i in range(4)]
            sqc = ptm_all[:, 0:1024]
            Rsqc = [bias["rptm"][0], bias["rptm"][1]]
            cmask = ckvn_flat
            Rmask = Res("cmask")
            smask = dsem()
            P.add("sp", lambda e: e.dma_start(out=cmask, in_=dr["c_mask"]), writes=[Rmask], dma_sem=smask)
            cm32 = cmask.bitcast(F32)
            o1 = cm32[:, 0:1024]
            o2 = cm32[:, 1024:2048]
            Ro = [Res("o1"), Res("o2")]
            qt = [pairbuf[:, i * 2048:(i + 1) * 2048] for i in range(2)]
            kt_ = [pairbuf[:, 4096 + i * 2048:4096 + (i + 1) * 2048] for i in range(2)]

            _hm = np.asarray(_host_consts()[1]).astype(np.float32)

            def mask_active(qh, n, kt):
                j0 = qh * 1024 + n * 512 - 128 * kt + 1920
                return bool(_hm[:, j0:j0 + 512].any())

            def put_aug(hh, row0, j):
                P.add("sp", lambda e, hh=hh, row0=row0, j=j: e.dma_start(out=qt[hh][row0:row0 + 4, :], in_=augq[4 * j:4 * j + 4, :]),
                      reads=[Raug], writes=[Rq[hh]], dma_sem=saug_q[hh])
                P.add("sp", lambda e, hh=hh, row0=row0, j=j: e.dma_start(out=kt_[hh][row0:row0 + 4, :], in_=augk[4 * j:4 * j + 4, :]),
                      reads=[Raug], writes=[Rk[hh]], dma_sem=saug_k[hh])

            def next_weights(i):
                cur = wnext.pop(0)
                if i + 1 < len(wlist):
                    wnext.append(load_w(wlist[i + 1]))
                return cur

            for pr in range(2):
                mb = cnt["mix"] % 2
                cnt["mix"] += 1
                wv, rw = next_weights(pr)
                for (dl, rl, wc0, scl) in ((qt, Rq, 0, 0.125), (kt_, Rk, 128, 1.0)):
                    for th in range(2):
                        sl, rsl = proj_psum(wv, rw, wc0, 128, th)
                        tsl = slice(th * 1024, (th + 1) * 1024)
                        for hh in range(2):
                            P.add("act", lambda e, dl=dl, sl=sl, tsl=tsl, scl=scl, hh=hh: e.activation(
                                dl[hh][0:64, tsl], sl[hh * 64:(hh + 1) * 64, :], AF.Copy, scale=scl),
                                reads=[rsl], writes=[rl[hh]])
                for hh in range(2):
                    P.add("dve", lambda e, hh=hh: e.memset(qt[hh][64:96, :], 0.0), writes=[Rq[hh]])
                    P.add("dve", lambda e, hh=hh: e.memset(kt_[hh][64:96, :], 0.0), writes=[Rk[hh]])
                    put_aug(hh, 64, 2 * pr + hh)
                proj_v(wv, rw, [256, 320])
                wout_drain()
                for hh in range(2):
                    h = 2 * pr + hh
                    ba = dict(bias)
                    ba["slope"] = 2.0 ** (-(2 * h + 1))
                    ba["mask"] = cmask
                    ba["rmask"] = Rmask
                    ba["active"] = mask_active
                    attend(kt_[hh][0:96, :], Rk[hh], qt[hh][0:96, :], Rq[hh], hh, ba, simple_fin(mb, hh))
                wout_later(pr, mb)

            wout_drain()
            P.barrier()
            sc_c = 32.0 ** -0.5
            CB = 1440
            for pr in range(2):
                mb = cnt["mix"] % 2
                cnt["mix"] += 1
                wv, rw = next_weights(2 + pr)
                for hh in range(2):
                    for (dst_t, rdst) in ((qt[hh], Rq[hh]), (kt_[hh], Rk[hh])):
                        P.add("dve", lambda e, dst_t=dst_t: e.memset(dst_t[32:64, :], 0.0), writes=[rdst])
                        P.add("dve", lambda e, dst_t=dst_t: e.memset(dst_t[64:128, :], 0.0), writes=[rdst])
                for (dl, rl, wc0, scl) in ((qt, Rq, 0, sc_c), (kt_, Rk, 128, 1.0)):
                    for th in range(2):
                        sl, rsl = proj_psum(wv, rw, wc0, 128, th)
                        tsl = slice(th * 1024, (th + 1) * 1024)
                        for hh in range(2):
                            for mp in range(2):
                                P.add("act", lambda e, dl=dl, sl=sl, tsl=tsl, scl=scl, hh=hh, mp=mp: e.activation(
                                    dl[hh][mp * 64:mp * 64 + 32, tsl], sl[hh * 64 + mp * 32:hh * 64 + mp * 32 + 32, :],
                                    AF.Copy, scale=scl), reads=[rsl], writes=[rl[hh]])
                for hh in range(2):
                    put_aug(hh, 32, 4 + 2 * pr + hh)
                    put_aug(hh, 96, 4 + 2 * pr + hh)
                proj_v(wv, rw, [256, 320])
                wout_drain()
                for hh in range(2):
                    h = 2 * pr + hh
                    bc_ = dict(bias)
                    bc_["slope"] = 2.0 ** (-(2 * h + 2))
                    bc_["mask"] = None

                    def fin_map(mp, hh=hh, mb=mb):
                        nr = slice(hh * 64, hh * 64 + 64)

                        def fin(qh, acc, racc):
                            recip_rows(acc, racc, hh, lambda: fin_rest(qh, acc, racc))

                        def fin_rest(qh, acc, racc):
                            od = o1 if mp == 0 else o2
                            P.add("dve", lambda e: e.tensor_tensor(od[nr, :], acc[nr, :], bcs[nr, :], ALU.mult),
                                  reads=[racc, Rbcs], writes=[Ro[mp]])
                            if mp == 1:
                                P.add("dve", lambda e: e.scalar_tensor_tensor(
                                    o1[nr, :], o2[nr, :], cst[nr, col["nlam_%d" % l]:col["nlam_%d" % l] + 1],
                                    o1[nr, :], ALU.mult, ALU.add), reads=[Ro[0], Ro[1], Rcst], writes=[Ro[0]])
                                P.add("dve", lambda e: e.tensor_tensor(sqc[nr, :], o1[nr, :], o1[nr, :], ALU.mult),
                                      reads=[Ro[0]], writes=[Rsqc])
                                pend.append({"fn": lambda: fin_norm(qh), "mask": None})

                        def fin_norm(qh):
                            for n in range(2):
                                sb, rsb = s_bank()
                                P.add("pe", lambda e, sb=sb, n=n: e.matmul(
                                    sb, ones_bf[nr, :], sqc[nr, n * 512:(n + 1) * 512], start=True, stop=True),
                                    reads=[Rsqc, Rconst], writes=[rsb])
                                P.add("act", lambda e, sb=sb, n=n: e.activation(
                                    o2[nr, n * 512:(n + 1) * 512], sb[nr, :], AF.Ln,
                                    bias=cst[nr, col["rc"] + 3:col["rc"] + 4], scale=1.0 / 64.0),
                                    reads=[rsb, Rcst], writes=[Ro[1]])
                            P.add("act", lambda e: e.activation(o2[nr, :], o2[nr, :], AF.Exp, scale=-0.5), reads=[Ro[1]], writes=[Ro[1]])
                            P.add("dve", lambda e: e.scalar_tensor_tensor(
                                mixed[mb][nr, qh * 1024:(qh + 1) * 1024], o1[nr, :],
                                cst[nr, col["gc_%d" % l]:col["gc_%d" % l] + 1], o2[nr, :], ALU.mult, ALU.mult),
                                reads=[Ro[0], Ro[1], Rcst], writes=[Rmixed[mb]])
                        return fin

                    for qh in range(2):
                        for mp in range(2):
                            attend(kt_[hh][mp * 64:mp * 64 + 64, :], Rk[hh], qt[hh][mp * 64:mp * 64 + 64, :], Rq[hh],
                                   hh, bc_, fin_map(mp), qhs=(qh,))
                wout_later(6 + pr, mb)
            wout_drain()
            P.barrier()
            A.release(m)

        def final():
            m = A.mark()
            gfin = A.alloc(D * 4, F32, "gfin")
            Rg = Res("gfin")
            sg_ = dsem()
            P.add("sp", lambda e: e.dma_start(out=gfin, in_=dr["final_norm"].partition_broadcast(128)),
                  writes=[Rg], dma_sem=sg_)
            yb = [A.alloc(D * 4, F32, "y%d" % i) for i in range(2)]
            Ry = [Res("y%d" % i) for i in range(2)]
            sy = [dsem(), dsem()]
            junk = A.alloc(D * 4, F32, "junk")
            Rjunk = Res("junk")
            ssq = A.alloc(32 * 4, F32, "ssq")
            Rssq = Res("ssq")
            for tt in range(16):
                h = tt // 8
                b = tt % 2
                sl, rsl = next_slot()
                for c in range(8):
                    P.add("pe", lambda e, sl=sl, c=c, tt=tt: e.transpose(sl[:, c * 128:(c + 1) * 128],
                                                                        xT[:, c, tt * 128:(tt + 1) * 128], ident),
                          reads=[RX[c][h], Rconst], writes=[rsl])
                sq_ = ssq[:, tt:tt + 1]
                P.add("act", lambda e, sl=sl, sq_=sq_: e.activation(junk, sl, AF.Square, accum_out=sq_),
                      reads=[rsl], writes=[Rjunk, Rssq])
                P.add("act", lambda e, sq_=sq_: e.activation(sq_, sq_, AF.Sqrt, bias=c_eps, scale=1.0 / D),
                      reads=[Rssq, Rcst], writes=[Rssq])
                P.add("dve", lambda e, sq_=sq_: e.reciprocal(sq_, sq_), reads=[Rssq], writes=[Rssq])
                P.add("dve", lambda e, sl=sl, sq_=sq_, b=b: e.scalar_tensor_tensor(yb[b], sl, sq_, gfin, ALU.mult, ALU.mult),
                      reads=[rsl, Rssq, Rg], writes=[Ry[b]])
                P.add("sp", lambda e, b=b, tt=tt: e.dma_start(out=out[tt * 128:(tt + 1) * 128, :], in_=yb[b]),
                      reads=[Ry[b]], dma_sem=sy[b])
            P.barrier()
            A.release(m)

        stages = []
        for l in range(DEPTH):
            stages += [("ffn", l, 1), ("mix", l, 0), ("ffn", l, 2)]
        nst = len(stages) if stop is None else stop
        for (kind, l, w) in stages[:nst]:
            if kind == "ffn":
                ffn(l, w)
            else:
                mixer(l)
        if debug:
            sd = dsem()
            P.add("sp", lambda e: e.dma_start(out=dbg, in_=xT), reads=[RX[c][h] for c in range(8) for h in range(2)],
                  dma_sem=sd)
            P.barrier()
        final()
        print("[kernel] ops=%d arena_peak=%d bytes dsems=%d" % (len(P.ops), A.peak * 4, dsem_i[0]))
        global _LASTP
        _LASTP = P
        P.emit(sems)
    return nc


_CACHE = {}


def kernel(**inputs):
    ident, cmask, rc = _host_consts()
    nc = build()
    shared = {k: np.ascontiguousarray(np.asarray(inputs[k], dtype=np.float32)) for k in PARAM_SHAPES}
    x = np.asarray(inputs["x"], dtype=np.float32)
    pos = np.asarray(inputs["positions"]).astype(np.int32)
    in_maps = []
    for b in range(8):
        m = dict(shared)
        m["x"] = np.ascontiguousarray(x[b])
        m["positions"] = np.ascontiguousarray(pos[b])
        m["c_ident"] = ident
        m["c_mask"] = cmask
        m["c_rc"] = rc
        in_maps.append(m)
    res = run_bass_kernel_spmd(nc, in_maps, core_ids=list(range(8)))
    return np.stack([np.asarray(r["out"], dtype=np.float32) for r in res.results], axis=0)
```

```python
import math
import contextlib
import numpy as np
import ml_dtypes
import concourse.bass as bass
import concourse.mybir as mybir
from concourse.bass_utils import run_bass_kernel_spmd

F32 = mybir.dt.float32
BF16 = mybir.dt.bfloat16
F16 = mybir.dt.float16
I32 = mybir.dt.int32
AF = mybir.ActivationFunctionType
ALU = mybir.AluOpType

SAME_ENGINE_SYNC = True

S = 2048
D = 1024
DFF = 2816
NIN = 2208
DEPTH = 2
EPS = 1e-6


class Res:
    __slots__ = ("name", "w", "r")

    def __init__(self, name=""):
        self.name = name
        self.w = None
        self.r = []


class Op:
    __slots__ = ("eng", "fn", "deps", "dma_sem", "dma_val", "sig", "need_sig", "id")


class Prog:
    ENGS = ("pe", "act", "dve", "pool", "sp")

    def __init__(self, nc):
        self.nc = nc
        self.ops = []
        self.last = {e: None for e in self.ENGS}
        self.dma_since_barrier = []
        self.dma_counts = {}

    def add(self, eng, fn, reads=(), writes=(), dma_sem=None):
        op = Op()
        op.id = len(self.ops)
        op.eng = eng
        op.fn = fn
        def _flat(xs):
            out_ = []
            for x_ in xs:
                if isinstance(x_, (list, tuple)):
                    out_.extend(_flat(x_))
                else:
                    out_.append(x_)
            return out_
        reads = _flat(reads)
        writes = _flat(writes)
        deps = set()
        for r in reads:
            if r.w is not None:
                deps.add(r.w)
        for w in writes:
            if w.w is not None:
                deps.add(w.w)
            deps.update(w.r)
        op.deps = deps
        op.dma_sem = dma_sem
        op.dma_val = None
        if dma_sem is not None:
            k = id(dma_sem)
            self.dma_counts[k] = self.dma_counts.get(k, 0) + 16
            op.dma_val = self.dma_counts[k]
            self.dma_since_barrier.append(op.id)
        op.sig = None
        op.need_sig = False
        self.ops.append(op)
        for r in reads:
            r.r.append(op.id)
        for w in writes:
            w.w = op.id
            w.r = []
        self.last[eng] = op.id
        return op

    def barrier(self):
        ids = [v for v in self.last.values() if v is not None] + list(self.dma_since_barrier)
        self.dma_since_barrier = []
        for e in self.ENGS:
            op = self.add(e, lambda eng: eng.nop())
            op.deps.update(i for i in ids if i != op.id)

    def emit(self, sems):
        nc = self.nc
        ops = self.ops
        for op in ops:
            for d in op.deps:
                p = ops[d]
                if p.dma_sem is not None:
                    continue
                if p.eng == op.eng and op.dma_sem is None:
                    if p.eng == "pe" or not SAME_ENGINE_SYNC:
                        continue
                p.need_sig = True
        cnt = {e: 0 for e in self.ENGS}
        for op in ops:
            if op.need_sig:
                cnt[op.eng] += 1
                op.sig = cnt[op.eng]
        per = {e: [o for o in ops if o.eng == e] for e in self.ENGS}

        def run(ename, eng):
            waited = {}
            for op in per[ename]:
                need = {}
                for d in op.deps:
                    p = ops[d]
                    if p.dma_sem is not None:
                        key = ("d", id(p.dma_sem))
                        if need.get(key, (None, 0))[1] < p.dma_val:
                            need[key] = (p.dma_sem, p.dma_val)
                    else:
                        if p.eng == ename and op.dma_sem is None:
                            if ename == "pe" or not SAME_ENGINE_SYNC:
                                continue
                        key = ("e", p.eng)
                        if need.get(key, (None, 0))[1] < p.sig:
                            need[key] = (sems[p.eng], p.sig)
                for key, (sem, val) in need.items():
                    if waited.get(key, 0) >= val:
                        continue
                    eng.wait_ge(sem, val)
                    waited[key] = val
                ins = op.fn(eng)
                if op.dma_sem is not None:
                    ins.then_inc(op.dma_sem, 16)
                elif op.need_sig:
                    ins.then_inc(sems[ename], 1)

        with nc.Block() as block:
            @block.tensor
            def _(e):
                run("pe", e)

            @block.scalar
            def _(e):
                run("act", e)

            @block.vector
            def _(e):
                run("dve", e)

            @block.gpsimd
            def _(e):
                run("pool", e)

            @block.sync
            def _(e):
                run("sp", e)


class Arena:
    def __init__(self, ap, nwords):
        self.ap = ap
        self.n = nwords
        self.top = 0
        self.peak = 0

    def alloc(self, nbytes, dtype, name=""):
        nw = (nbytes + 31) // 32 * 8
        off = self.top
        self.top += nw
        self.peak = max(self.peak, self.top)
        assert self.top <= self.n, ("SBUF arena overflow", name, self.top * 4)
        v = self.ap[:, off:off + nw]
        if dtype != F32:
            v = v.bitcast(dtype)
        return v

    def mark(self):
        return self.top

    def release(self, m):
        self.top = m


def _host_consts():
    ident = np.eye(128, dtype=np.float32)
    p = np.arange(128)[:, None]
    j = np.arange(4096)[None, :]
    dl = p - (j - 1920)
    c = (np.abs(dl) <= 64).astype(np.float32)
    c += ((dl % 4 == 0) & (np.abs(dl) <= 256)).astype(np.float32)
    c += ((dl % 16 == 0) & (np.abs(dl) <= 1024)).astype(np.float32)
    cmask = c.astype(ml_dtypes.bfloat16)
    half = 16
    inv = (np.float32(10000.0) ** (-np.arange(half, dtype=np.float32) / np.float32(half))).astype(np.float32)
    rc = np.zeros((128, 16), dtype=np.float32)
    pp = np.arange(128)
    rc[:, 0] = inv[pp % 16]
    rc[:, 1] = np.where((pp % 32) < 16, -1.0, 1.0)
    rc[:, 2] = -math.pi
    rc[:, 3] = EPS
    rc[:, 4] = 0.0
    slopes = [2.0 ** (-(2 * h + 1)) for h in range(4)] + [2.0 ** (-(2 * h + 2)) for h in range(4)]
    for j, sj in enumerate(slopes):
        for i in range(4):
            p_ = 4 * j + i
            rc[p_, 5] = sj if i == 0 else 0.0
            rc[p_, 6] = sj if i == 1 else 0.0
            rc[p_, 7] = 1.0 if i >= 2 else 0.0
            rc[p_, 8] = -sj if i == 2 else 0.0
            rc[p_, 9] = -sj if i == 3 else 0.0
            rc[p_, 10] = 1.0 if i < 2 else 0.0
    return ident, cmask, rc


PARAM_SHAPES = {
    "ffn1_norm": [DEPTH, D], "ffn1_w_gate": [DEPTH, D, DFF], "ffn1_w_up": [DEPTH, D, DFF],
    "ffn1_w_down": [DEPTH, DFF, D], "mix_norm": [DEPTH, D], "w_in": [DEPTH, D, NIN],
    "mla_q_norm": [DEPTH, 384], "mla_w_uq": [DEPTH, 384, 768], "mla_kv_norm": [DEPTH, 256],
    "mla_w_ukv": [DEPTH, 256, 1024], "diff_lambda_q1": [DEPTH, 32], "diff_lambda_k1": [DEPTH, 32],
    "diff_lambda_q2": [DEPTH, 32], "diff_lambda_k2": [DEPTH, 32], "diff_head_norm": [DEPTH, 64],
    "w_out": [DEPTH, D, D], "ffn2_norm": [DEPTH, D], "ffn2_w_gate": [DEPTH, D, DFF],
    "ffn2_w_up": [DEPTH, D, DFF], "ffn2_w_down": [DEPTH, DFF, D], "final_norm": [D],
}

ARENA_WORDS = 53200


def build(stop=None, debug=False):
    nc = bass.Bass("TRN2", target_bir_lowering=False)
    dr = {}
    dr["x"] = nc.dram_tensor("x", [S, D], F32, kind="ExternalInput").ap()
    dr["positions"] = nc.dram_tensor("positions", [S], I32, kind="ExternalInput").ap()
    for k, shp in PARAM_SHAPES.items():
        dr[k] = nc.dram_tensor(k, shp, F32, kind="ExternalInput").ap()
    dr["c_ident"] = nc.dram_tensor("c_ident", [128, 128], F32, kind="ExternalInput").ap()
    dr["c_mask"] = nc.dram_tensor("c_mask", [128, 4096], BF16, kind="ExternalInput").ap()
    dr["c_rc"] = nc.dram_tensor("c_rc", [128, 16], F32, kind="ExternalInput").ap()
    out = nc.dram_tensor("out", [S, D], F32, kind="ExternalOutput").ap()
    dbg = None
    dbgm = None
    if debug:
        dbg = nc.dram_tensor("dbg", [128, 8, S], F32, kind="ExternalOutput").ap()
        dbgm = nc.dram_tensor("dbgm", [DEPTH, 8, 128, S], BF16, kind="ExternalOutput").ap()

    ddumps = {}

    def ddump(P, name, ap, shape, dtype, reads, sem):
        if not debug or name in ddumps:
            return
        t = nc.dram_tensor(name, list(shape), dtype, kind="ExternalOutput").ap()
        ddumps[name] = t
        P.add("sp", lambda e: e.dma_start(out=t, in_=ap), reads=reads, dma_sem=sem)

    with contextlib.ExitStack() as es:
        arena_t = es.enter_context(nc.sbuf_tensor("arena", [128, ARENA_WORDS], F32))
        ps = es.enter_context(nc.psum_tensor("ps", [128, 4096], F32))
        sems = {e: es.enter_context(nc.semaphore("s_" + e)) for e in Prog.ENGS}
        dsem_pool = [es.enter_context(nc.semaphore("d%d" % i)) for i in range(64)]
        dsem_i = [0]

        def dsem():
            s_ = dsem_pool[dsem_i[0]]
            dsem_i[0] += 1
            return s_

        P = Prog(nc)
        A = Arena(arena_t, ARENA_WORDS)

        SL = [ps[:, i * 1024:(i + 1) * 1024] for i in range(4)]
        RB = [Res("bank%d" % i) for i in range(8)]
        RSL = [[RB[2 * i], RB[2 * i + 1]] for i in range(4)]
        SBK = [ps[:, i * 512:(i + 1) * 512] for i in range(4)]
        slot_i = [0]

        def next_slot():
            i = slot_i[0] % 4
            slot_i[0] += 1
            return SL[i], RSL[i]

        sslot_i = [0]
        aslot_i = [0]

        def s_bank():
            i = sslot_i[0] % 4
            sslot_i[0] += 1
            return SBK[i], RB[i]

        def acc_slot():
            i = 2 + aslot_i[0] % 2
            aslot_i[0] += 1
            return SL[i], RSL[i]

        xT = A.alloc(8 * S * 4, F32, "xT").rearrange("p (c t) -> p c t", c=8)
        RX = [[Res("x%d_%d" % (c, h)) for h in range(2)] for c in range(8)]
        cst = A.alloc(384 * 4, F32, "cst")
        Rcst = Res("cst")
        ones_bf = A.alloc(128 * 2, BF16, "ones_bf")
        ident = A.alloc(128 * 4, F32, "ident")
        Rconst = Res("const")
        sem_c = dsem()

        col = {}
        cc = [0]

        def ccol(name, n):
            col[name] = cc[0]
            cc[0] += n

        for l in range(DEPTH):
            ccol("g1_%d" % l, 8); ccol("gm_%d" % l, 8); ccol("g2_%d" % l, 8)
            ccol("gq_%d" % l, 3); ccol("gkv_%d" % l, 2); ccol("gdh_%d" % l, 1)
            ccol("lq1_%d" % l, 32); ccol("lk1_%d" % l, 32); ccol("lq2_%d" % l, 32); ccol("lk2_%d" % l, 32)
            ccol("nlam_%d" % l, 1); ccol("gc_%d" % l, 1); ccol("t1_%d" % l, 1); ccol("t2_%d" % l, 1)
        ccol("rc", 16)
        ccol("scr", 32)
        assert cc[0] <= 384, cc[0]

        def cs(name, n=1, off=0):
            return cst[:, col[name] + off:col[name] + off + n]

        cst_parts = []

        def small_dma(out_ap, in_ap):
            r_ = Res("cstp")
            cst_parts.append(r_)
            P.add("sp", lambda e, o=out_ap, i=in_ap: e.dma_start(out=o, in_=i, allow_slow_non_contiguous=True),
                  writes=[r_], dma_sem=sem_c)

        for l in range(DEPTH):
            small_dma(cs("g1_%d" % l, 8), dr["ffn1_norm"][l].rearrange("(c p) -> p c", p=128))
            small_dma(cs("gm_%d" % l, 8), dr["mix_norm"][l].rearrange("(c p) -> p c", p=128))
            small_dma(cs("g2_%d" % l, 8), dr["ffn2_norm"][l].rearrange("(c p) -> p c", p=128))
            small_dma(cs("gq_%d" % l, 3), dr["mla_q_norm"][l].rearrange("(c p) -> p c", p=128))
            small_dma(cs("gkv_%d" % l, 2), dr["mla_kv_norm"][l].rearrange("(c p) -> p c", p=128))
            g64 = dr["diff_head_norm"][l].rearrange("(p o) -> p o", o=1)
            small_dma(cst[0:64, col["gdh_%d" % l]:col["gdh_%d" % l] + 1], g64)
            small_dma(cst[64:128, col["gdh_%d" % l]:col["gdh_%d" % l] + 1], g64)
            for nm, key in (("lq1", "diff_lambda_q1"), ("lk1", "diff_lambda_k1"),
                            ("lq2", "diff_lambda_q2"), ("lk2", "diff_lambda_k2")):
                small_dma(cs("%s_%d" % (nm, l), 32), dr[key][l].partition_broadcast(128))
        small_dma(cs("rc", 16), dr["c_rc"])
        P.add("dve", lambda e: e.memset(cs("scr", 32), 0.0), reads=cst_parts, writes=[Rcst])
        sem_id = dsem()
        P.add("sp", lambda e: e.dma_start(out=ident, in_=dr["c_ident"]), writes=[Rconst], dma_sem=sem_id)
        P.add("dve", lambda e: e.memset(ones_bf, 1.0), writes=[Rconst])
        c_inv = cs("rc", 1, 0); c_sign = cs("rc", 1, 1); c_negpi = cs("rc", 1, 2); c_eps = cs("rc", 1, 3)
        c_zero = cs("rc", 1, 4)

        for l in range(DEPTH):
            lam_init = 0.8 - 0.6 * math.exp(-0.3 * l)
            scr = cs("scr", 32)
            for a_, b_, t_ in (("lq1", "lk1", "t1"), ("lq2", "lk2", "t2")):
                P.add("dve", lambda e, a_=a_, b_=b_, l=l: e.tensor_tensor(
                    scr, cs("%s_%d" % (a_, l), 32), cs("%s_%d" % (b_, l), 32), ALU.mult),
                    reads=[Rcst], writes=[Rcst])
                P.add("dve", lambda e, t_=t_, l=l: e.reduce_sum(cs("%s_%d" % (t_, l)), scr, mybir.AxisListType.X),
                      reads=[Rcst], writes=[Rcst])
                P.add("act", lambda e, t_=t_, l=l: e.activation(cs("%s_%d" % (t_, l)), cs("%s_%d" % (t_, l)), AF.Exp),
                      reads=[Rcst], writes=[Rcst])
            P.add("dve", lambda e, l=l, li=lam_init: e.scalar_tensor_tensor(
                cs("nlam_%d" % l), cs("t2_%d" % l), -li, cs("t1_%d" % l), ALU.add, ALU.subtract),
                reads=[Rcst], writes=[Rcst])
            P.add("dve", lambda e, l=l, li=lam_init: e.tensor_scalar(
                cs("gc_%d" % l), cs("gdh_%d" % l), 1.0 - li, None, ALU.mult),
                reads=[Rcst], writes=[Rcst])

        m0 = A.mark()
        xin = [A.alloc(D * 4, F32, "xin%d" % i) for i in range(2)]
        Rxin = [Res("xin%d" % i) for i in range(2)]
        sxin = [dsem(), dsem()]
        for tt in range(16):
            b = tt % 2
            P.add("sp", lambda e, tt=tt, b=b: e.dma_start(out=xin[b], in_=dr["x"][tt * 128:(tt + 1) * 128, :]),
                  writes=[Rxin[b]], dma_sem=sxin[b])
            sl, rsl = next_slot()
            for c in range(8):
                P.add("pe", lambda e, sl=sl, c=c, b=b: e.transpose(sl[:, c * 128:(c + 1) * 128],
                                                                  xin[b][:, c * 128:(c + 1) * 128], ident),
                      reads=[Rxin[b], Rconst], writes=[rsl])
            h = tt // 8
            eng = "act" if tt % 2 == 0 else "dve"
            dst = xT[:, :, tt * 128:(tt + 1) * 128]
            src = sl.rearrange("p (c t) -> p c t", c=8)
            if eng == "act":
                P.add("act", lambda e, dst=dst, src=src: e.copy(dst, src), reads=[rsl],
                      writes=[RX[c][h] for c in range(8)])
            else:
                P.add("dve", lambda e, dst=dst, src=src: e.tensor_copy(dst, src), reads=[rsl],
                      writes=[RX[c][h] for c in range(8)])
        P.barrier()
        A.release(m0)

        def rms_feature(gname, hT, RH, tmp_sq, Rsq, rs, Rrs):
            slots = {}

            def stage1(tc):
                h = tc // 2
                tsl = slice(tc * 512, (tc + 1) * 512)
                sl, rsl = next_slot()
                slots[tc] = (sl, rsl)
                for c in range(8):
                    P.add("act", lambda e, c=c, tsl=tsl: e.activation(tmp_sq[:, c, :], xT[:, c, tsl], AF.Square),
                          reads=[RX[c][h]], writes=[Rsq[c]])
                    P.add("pe", lambda e, c=c, sl=sl: e.matmul(sl[:, 0:512], ones_bf, tmp_sq[:, c, :], start=(c == 0), stop=(c == 7)),
                          reads=[Rsq[c], Rconst], writes=[rsl])

            def stage2(tc):
                h = tc // 2
                tsl = slice(tc * 512, (tc + 1) * 512)
                sl, rsl = slots[tc]
                P.add("act", lambda e, sl=sl, tsl=tsl: e.activation(rs[:, tsl], sl[:, 0:512], AF.Ln, bias=c_eps, scale=1.0 / D),
                      reads=[rsl, Rcst], writes=[Rrs[tc]])
                P.add("act", lambda e, tsl=tsl: e.activation(rs[:, tsl], rs[:, tsl], AF.Exp, scale=-0.5), reads=[Rrs[tc]], writes=[Rrs[tc]])
                for c in range(8):
                    P.add("dve", lambda e, c=c, tsl=tsl: e.scalar_tensor_tensor(
                        hT[:, c, tsl], xT[:, c, tsl], cs(gname, 1, c), rs[:, tsl], ALU.mult, ALU.mult),
                        reads=[RX[c][h], Rrs[tc], Rcst], writes=[RH[tc]])

            stage1(0)
            for tc in range(4):
                if tc + 1 < 4:
                    stage1(tc + 1)
                stage2(tc)

        def ffn(l, which):
            m = A.mark()
            hT = A.alloc(8 * S * 2, BF16, "hT").rearrange("p (c t) -> p c t", c=8)
            RH = [Res("h%d" % i) for i in range(4)]
            aT = A.alloc(12 * S * 2, BF16, "aT").rearrange("p (c t) -> p c t", c=12)
            RA = [[Res("a%d_%d" % (c, h)) for h in range(2)] for c in range(12)]
            wgu = [A.alloc(2 * 8 * 256 * 2, BF16, "wgu%d" % i).rearrange("p (g c f) -> p g c f", g=2, c=8) for i in range(2)]
            Rwgu = [Res("wgu%d" % i) for i in range(2)]
            swgu = [dsem() for _ in range(2)]
            wd = [A.alloc(12 * 128 * 2, BF16, "wd%d" % i).rearrange("p (c f) -> p c f", c=12) for i in range(3)]
            Rwd = [Res("wd%d" % i) for i in range(3)]
            swd = [dsem() for _ in range(3)]
            sg = [A.alloc(1024 * 4, F32, "sg%d" % i) for i in range(2)]
            Rsg = [Res("sg%d" % i) for i in range(2)]
            tmp_sq = A.alloc(8 * 512 * 2, BF16, "sq").rearrange("p (c t) -> p c t", c=8)
            Rsq = [Res("sq%d" % c) for c in range(8)]
            rs = A.alloc(S * 4, F32, "rs")
            Rrs = [Res("rs%d" % i) for i in range(4)]
            pre = "ffn%d_" % which
            wg_d, wu_d, wd_d = dr[pre + "w_gate"][l], dr[pre + "w_up"][l], dr[pre + "w_down"][l]
            rms_feature("g%d_%d" % (which, l), hT, RH, tmp_sq, Rsq, rs, Rrs)
            wcnt = 0
            dcnt = 0
            scnt = 0
            for (f0, nfc) in ((0, 12), (12, 10)):
                for pr in range(nfc // 2):
                    s_ = wcnt % 2
                    wcnt += 1
                    fa = (f0 + 2 * pr) * 128
                    P.add("pool", lambda e, s_=s_, fa=fa: e.dma_start(
                        out=wgu[s_][:, 0], in_=wg_d[:, fa:fa + 256].rearrange("(c p) f -> p c f", p=128)),
                        writes=[Rwgu[s_]], dma_sem=swgu[s_])
                    P.add("pool", lambda e, s_=s_, fa=fa: e.dma_start(
                        out=wgu[s_][:, 1], in_=wu_d[:, fa:fa + 256].rearrange("(c p) f -> p c f", p=128)),
                        writes=[Rwgu[s_]], dma_sem=swgu[s_])
                    for j in range(2):
                        fcl = 2 * pr + j
                        for th in range(2):
                            gsl, rg = next_slot()
                            usl, ru = next_slot()
                            for (g_, sl_, r_) in ((0, gsl, rg), (1, usl, ru)):
                                for c in range(8):
                                    for n in range(2):
                                        tq = th * 2 + n
                                        P.add("pe", lambda e, g_=g_, sl_=sl_, c=c, n=n, s_=s_, j=j, tq=tq: e.matmul(
                                            sl_[:, n * 512:(n + 1) * 512], wgu[s_][:, g_, c, j * 128:(j + 1) * 128],
                                            hT[:, c, tq * 512:(tq + 1) * 512], start=(c == 0), stop=(c == 7)),
                                            reads=[Rwgu[s_], RH[tq]], writes=[r_])
                            b = scnt % 2
                            scnt += 1
                            P.add("act", lambda e, b=b, gsl=gsl: e.activation(sg[b], gsl, AF.Silu),
                                  reads=[rg], writes=[Rsg[b]])
                            P.add("dve", lambda e, b=b, usl=usl, fcl=fcl, th=th: e.tensor_tensor(
                                aT[:, fcl, th * 1024:(th + 1) * 1024], sg[b], usl, ALU.mult),
                                reads=[Rsg[b], ru], writes=[RA[fcl][th]])
                for dc in range(8):
                    s_ = dcnt % 3
                    dcnt += 1
                    P.add("pool", lambda e, s_=s_, dc=dc, f0=f0, nfc=nfc: e.dma_start(
                        out=wd[s_][:, 0:nfc, :],
                        in_=wd_d[f0 * 128:(f0 + nfc) * 128, dc * 128:(dc + 1) * 128].rearrange("(c p) f -> p c f", p=128)),
                        writes=[Rwd[s_]], dma_sem=swd[s_])
                    for th in range(2):
                        sl, rsl = next_slot()
                        for fcl in range(nfc):
                            for n in range(2):
                                P.add("pe", lambda e, sl=sl, fcl=fcl, n=n, s_=s_, th=th, nfc=nfc: e.matmul(
                                    sl[:, n * 512:(n + 1) * 512], wd[s_][:, fcl, :],
                                    aT[:, fcl, th * 1024 + n * 512:th * 1024 + (n + 1) * 512],
                                    start=(fcl == 0), stop=(fcl == nfc - 1)),
                                    reads=[Rwd[s_], RA[fcl][th]], writes=[rsl])
                        xs = xT[:, dc, th * 1024:(th + 1) * 1024]
                        P.add("dve", lambda e, sl=sl, xs=xs: e.scalar_tensor_tensor(xs, sl, 0.5, xs, ALU.mult, ALU.add),
                              reads=[rsl, RX[dc][th]], writes=[RX[dc][th]])
            P.barrier()
            A.release(m)

        def mixer(l):
            m = A.mark()
            RH = [Res("h%d" % i) for i in range(4)]
            cqn_flat = A.alloc(3 * S * 2, BF16, "cqn")
            cqn = cqn_flat.rearrange("p (c t) -> p c t", c=3)
            Rcqn = [Res("cqn%d" % h) for h in range(2)]
            ckvn_flat = A.alloc(2 * S * 2, BF16, "ckvn")
            ckvn = ckvn_flat.rearrange("p (c t) -> p c t", c=2)
            Rckvn = [Res("ckvn%d" % h) for h in range(2)]
            krope = A.alloc(S * 2, BF16, "krope")
            Rkrope = Res("krope")
            w_in = dr["w_in"][l]
            pairbuf = A.alloc(8192 * 2 + 16 * 192 * 2, BF16, "pair")
            Rq = [Res("q%d" % i) for i in range(2)]
            Rk = [Res("k%d" % i) for i in range(2)]
            Rv = Res("v")
            vaug = pairbuf[:, 8192:8192 + 16 * 192].rearrange("p (k e) -> p k e", k=16)
            mixedall = A.alloc(2 * S * 2, BF16, "mixedall")
            mixed = [mixedall[:, i * S:(i + 1) * S] for i in range(2)]
            Rmixed = [Res("mixed%d" % i) for i in range(2)]
            wout = [A.alloc(D * 2, BF16, "wout%d" % i) for i in range(2)]
            Rwout = [Res("wout%d" % i) for i in range(2)]
            swout = [dsem(), dsem()]
            NPT = 5
            PTall = A.alloc(NPT * 512 * 2, BF16, "PT")
            RPT6 = [Res("PT%d" % i) for i in range(NPT)]
            PT6 = [PTall[:, i * 512:(i + 1) * 512] for i in range(NPT)]
            bcrr = A.alloc(2048 * 4, F32, "bcrr")
            bcs = bcrr[:, 0:1024]
            Rbcs = Res("bcs")
            rrow = bcrr[:, 1024:2048]
            Rrrow = Res("rrow")
            Rscr = Res("scr")
            ropeA, RropeA = bcs, Rbcs
            ropeB, RropeB = rrow, Rrrow
            wst = [A.alloc(8 * 384 * 2, BF16, "wst%d" % i) for i in range(2)]
            Rwst = [Res("wst%d" % i) for i in range(2)]
            swst = [dsem(), dsem()]
            cnt = {"pt": 0, "wst": 0, "mix": 0, "tmp": 0, "dist": 0, "ptm": 0}
            hT = A.alloc(8 * S * 2, BF16, "hT").rearrange("p (c t) -> p c t", c=8)
            m1 = A.mark()
            tmp_sq = A.alloc(8 * 512 * 2, BF16, "sq").rearrange("p (c t) -> p c t", c=8)
            Rsq = [Res("sq%d" % c) for c in range(8)]
            rs = A.alloc(S * 4, F32, "rs")
            Rrs = [Res("rs%d" % i) for i in range(4)]
            rms_feature("gm_%d" % l, hT, RH, tmp_sq, Rsq, rs, Rrs)
            P.barrier()
            A.release(m1)


            def load_w(cols_list):
                b = cnt["wst"] % 2
                cnt["wst"] += 1
                tot = sum(n for _, n in cols_list)
                v = wst[b][:, 0:8 * tot].rearrange("p (c f) -> p c f", c=8)
                o = 0
                for (c0, n) in cols_list:
                    P.add("pool", lambda e, v=v, o=o, c0=c0, n=n: e.dma_start(
                        out=v[:, :, o:o + n], in_=w_in[:, c0:c0 + n].rearrange("(c p) f -> p c f", p=128)),
                        writes=[Rwst[b]], dma_sem=swst[b])
                    o += n
                return v, Rwst[b]

            def proj_fm(wv, rw, wc0, M, dst_fn, dst_res_fn, scale, src=None, rsrc=None, nck=8):
                src = hT if src is None else src
                for th in range(2):
                    sl, rsl = next_slot()
                    for c in range(nck):
                        for n in range(2):
                            tq = th * 2 + n
                            rr = RH[tq] if rsrc is None else rsrc[th]
                            P.add("pe", lambda e, sl=sl, c=c, n=n, tq=tq: e.matmul(
                                sl[0:M, n * 512:(n + 1) * 512], wv[:, c, wc0:wc0 + M],
                                src[:, c, tq * 512:(tq + 1) * 512], start=(c == 0), stop=(c == nck - 1)),
                                reads=[rw, rr], writes=[rsl])
                    d_ = dst_fn(th)
                    P.add("act", lambda e, d_=d_, sl=sl: e.activation(d_, sl[0:M, :], AF.Copy, scale=scale),
                          reads=[rsl], writes=[dst_res_fn(th)])

            def proj_psum(wv, rw, wc0, M, th, src=None, rsrc=None, nck=8):
                src = hT if src is None else src
                sl, rsl = next_slot()
                for c in range(nck):
                    for n in range(2):
                        tq = th * 2 + n
                        rr = RH[tq] if rsrc is None else rsrc[th]
                        P.add("pe", lambda e, sl=sl, c=c, n=n, tq=tq: e.matmul(
                            sl[0:M, n * 512:(n + 1) * 512], wv[:, c, wc0:wc0 + M],
                            src[:, c, tq * 512:(tq + 1) * 512], start=(c == 0), stop=(c == nck - 1)),
                            reads=[rw, rr], writes=[rsl])
                return sl, rsl

            def proj_v(wv, rw, cols, src=None, rsrc=None, nck=8):
                src = hT if src is None else src
                P.add("dve", lambda e: e.memset(vaug[:, :, 64:128], 1.0), writes=[Rv])
                for k4 in range(4):
                    sl, rsl = next_slot()
                    for kk in range(4):
                        kt = k4 * 4 + kk
                        for hh in range(2):
                            for c in range(nck):
                                rr = RH[kt // 4] if rsrc is None else rsrc[kt // 8]
                                P.add("pe", lambda e, sl=sl, kk=kk, hh=hh, c=c, kt=kt: e.matmul(
                                    sl[:, kk * 128 + hh * 64:kk * 128 + hh * 64 + 64],
                                    src[:, c, kt * 128:(kt + 1) * 128], wv[:, c, cols[hh]:cols[hh] + 64],
                                    start=(c == 0), stop=(c == nck - 1)),
                                    reads=[rw, rr], writes=[rsl])
                    for hv in range(2):
                        P.add("act", lambda e, sl=sl, k4=k4, hv=hv: e.copy(
                            vaug[:, k4 * 4:(k4 + 1) * 4, hv * 128:hv * 128 + 64],
                            sl[:, 0:512].rearrange("p (k h e) -> p k h e", k=4, h=2)[:, :, hv, :]),
                            reads=[rsl], writes=[Rv])

            sdbgm = dsem() if dbgm is not None else None

            wq = []

            def wout_later(chunk, mb):
                flush_pv()
                wq.append((chunk, mb))

            def wout_drain():
                while len(wq) >= 2:
                    wout_apply2(wq.pop(0), wq.pop(0))

            def wout_apply2(a_, b_):
                flush_pv()
                items = (a_, b_)
                for (chunk, mb) in items:
                    b = chunk % 2
                    if dbgm is not None:
                        P.add("sp", lambda e, chunk=chunk, mb=mb: e.dma_start(out=dbgm[l, chunk], in_=mixed[mb]),
                              reads=[Rmixed[mb]], dma_sem=sdbgm)
                    P.add("pool", lambda e, b=b, chunk=chunk: e.dma_start(
                        out=wout[b], in_=dr["w_out"][l][chunk * 128:(chunk + 1) * 128, :]),
                        writes=[Rwout[b]], dma_sem=swout[b])
                for dc in range(8):
                    for th in range(2):
                        sl, rsl = next_slot()
                        for j, (chunk, mb) in enumerate(items):
                            b = chunk % 2
                            for n in range(2):
                                P.add("pe", lambda e, sl=sl, n=n, dc=dc, th=th, b=b, mb=mb, j=j: e.matmul(
                                    sl[:, n * 512:(n + 1) * 512], wout[b][:, dc * 128:(dc + 1) * 128],
                                    mixed[mb][:, th * 1024 + n * 512:th * 1024 + (n + 1) * 512],
                                    start=(j == 0), stop=(j == 1)),
                                    reads=[Rwout[b], Rmixed[mb]], writes=[rsl])
                        xs = xT[:, dc, th * 1024:(th + 1) * 1024]
                        P.add("dve", lambda e, sl=sl, xs=xs: e.tensor_tensor(xs, sl, xs, ALU.add),
                              reads=[rsl, RX[dc][th]], writes=[RX[dc][th]])

            def wout_apply(chunk, mb):
                flush_pv()
                b = chunk % 2
                if dbgm is not None:
                    P.add("sp", lambda e, chunk=chunk, mb=mb: e.dma_start(out=dbgm[l, chunk], in_=mixed[mb]),
                          reads=[Rmixed[mb]], dma_sem=sdbgm)
                P.add("pool", lambda e, b=b, chunk=chunk: e.dma_start(
                    out=wout[b], in_=dr["w_out"][l][chunk * 128:(chunk + 1) * 128, :]),
                    writes=[Rwout[b]], dma_sem=swout[b])
                for dc in range(8):
                    for th in range(2):
                        sl, rsl = next_slot()
                        for n in range(2):
                            P.add("pe", lambda e, sl=sl, n=n, dc=dc, th=th, b=b: e.matmul(
                                sl[:, n * 512:(n + 1) * 512], wout[b][:, dc * 128:(dc + 1) * 128],
                                mixed[mb][:, th * 1024 + n * 512:th * 1024 + (n + 1) * 512], start=True, stop=True),
                                reads=[Rwout[b], Rmixed[mb]], writes=[rsl])
                        xs = xT[:, dc, th * 1024:(th + 1) * 1024]
                        P.add("dve", lambda e, sl=sl, xs=xs: e.tensor_tensor(xs, sl, xs, ALU.add),
                              reads=[rsl, RX[dc][th]], writes=[RX[dc][th]])

            pend = []
            LA = 3

            pexp = []
            EXPD = 1

            def flush_pv(keep=0):
                if keep == 0:
                    while pexp:
                        pexp.pop(0)()
                while pend:
                    ntile_after = sum(1 for j in pend[1:] if j.get("tile"))
                    if ntile_after < keep:
                        break
                    job = pend.pop(0)
                    if job["mask"] is not None:
                        job["mask"]()
                    job["fn"]()

            def attend(kT, rk, qT, rq, hh, bias, fin, qhs=(0, 1)):
                if bias is not None:
                    P.add("dve", lambda e: e.tensor_scalar(bias["posqh"], bias["posq"], 2.0 * float(bias["slope"]), None, ALU.mult),
                          reads=[bias["rpos"]], writes=[bias["rposh"]])
                    P.add("dve", lambda e: e.tensor_scalar(bias["poskh"], bias["posk"], 2.0 * float(bias["slope"]), None, ALU.mult),
                          reads=[bias["rpos"]], writes=[bias["rposh"]])
                for qh in qhs:
                    acc, racc = acc_slot()
                    act_fn = (bias or {}).get("active")
                    tiles = [(kt, n) for kt in range(16) for n in range(2) if act_fn is None or act_fn(qh, n, kt)]
                    first_kt = {n: min(kt for kt, n_ in tiles if n_ == n) for n in range(2)}
                    last_kt = {n: max(kt for kt, n_ in tiles if n_ == n) for n in range(2)}
                    rready = {}

                    def emit_r(kt_, n_):
                        db_ = cnt["dist"] % 4
                        cnt["dist"] += 1
                        q0_ = qh * 1024 + n_ * 512
                        P.add("dve", lambda e: e.tensor_scalar(
                            bias["dist"][db_], bias["posqh"][:, q0_:q0_ + 512], bias["poskh"][:, kt_:kt_ + 1],
                            0.0, ALU.subtract, ALU.max),
                            reads=[bias["rposh"]], writes=[bias["rdist"][db_]])
                        rready[(kt_, n_)] = db_

                    for ti, (kt, n) in enumerate(tiles):
                        if True:
                            q0 = qh * 1024 + n * 512
                            if pend and pend[0]["mask"] is not None and sum(1 for j in pend[1:] if j.get("tile")) >= LA - 1:
                                pend[0]["mask"]()
                                pend[0]["mask"] = None
                            sb, rsb = s_bank()
                            P.add("pe", lambda e, sb=sb, kt=kt, q0=q0: e.matmul(
                                sb, kT[:, kt * 128:(kt + 1) * 128], qT[:, q0:q0 + 512], start=True, stop=True),
                                reads=[rk, rq], writes=[rsb])
                            pb = cnt["pt"] % NPT
                            cnt["pt"] += 1
                            mask_job = None
                            if bias is None:
                                P.add("act", lambda e, pb=pb, sb=sb: e.activation(PT6[pb], sb, AF.Exp),
                                      reads=[rsb], writes=[RPT6[pb]])
                                pv_src, pv_res = PT6[pb], RPT6[pb]
                            else:
                                if (kt, n) not in rready:
                                    emit_r(kt, n)
                                db = rready[(kt, n)]
                                tb = cnt["tmp"] % 4
                                cnt["tmp"] += 1
                                via_act = False
                                if via_act:
                                    P.add("act", lambda e, tb=tb, sb=sb: e.copy(bias["tmp"][tb], sb),
                                          reads=[rsb], writes=[bias["rtmp"][tb]])
                                while len(pexp) >= EXPD:
                                    pexp.pop(0)()
                                if ti + 1 < len(tiles):
                                    emit_r(*tiles[ti + 1])
                                if via_act:
                                    P.add("dve", lambda e, db=db, tb=tb: e.tensor_tensor(
                                        bias["tmp"][tb], bias["tmp"][tb], bias["dist"][db], ALU.subtract),
                                        reads=[bias["rdist"][db], bias["rtmp"][tb]], writes=[bias["rtmp"][tb]])
                                else:
                                    P.add("dve", lambda e, db=db, tb=tb, sb=sb: e.tensor_tensor(
                                        bias["tmp"][tb], sb, bias["dist"][db], ALU.subtract),
                                        reads=[bias["rdist"][db], rsb], writes=[bias["rtmp"][tb]])

                                def exp_job(pb=pb, tb=tb):
                                    P.add("act", lambda e: e.activation(PT6[pb], bias["tmp"][tb], AF.Exp),
                                          reads=[bias["rtmp"][tb]], writes=[RPT6[pb]])
                                pexp.append(exp_job)
                                pv_src, pv_res = PT6[pb], RPT6[pb]
                                if bias.get("mask") is not None:
                                    mb_ = cnt["ptm"] % 4
                                    cnt["ptm"] += 1
                                    j0 = q0 - 128 * kt + 1920

                                    def mask_job(mb_=mb_, pb=pb, j0=j0):
                                        P.add("dve", lambda e: e.tensor_tensor(
                                            bias["ptm"][mb_], PT6[pb], bias["mask"][:, j0:j0 + 512], ALU.mult),
                                            reads=[RPT6[pb], bias["rmask"]], writes=[bias["rptm"][mb_]])
                                    pv_src, pv_res = bias["ptm"][mb_], bias["rptm"][mb_]

                            def pv_job(acc=acc, racc=racc, kt=kt, n=n, qh=qh, pv_src=pv_src, pv_res=pv_res, first_kt=first_kt, last_kt=last_kt, tiles=tiles):
                                P.add("pe", lambda e: e.matmul(
                                    acc[:, n * 512:(n + 1) * 512], vaug[:, kt, hh * 64:hh * 64 + 128], pv_src,
                                    start=(kt == first_kt[n]), stop=(kt == last_kt[n])),
                                    reads=[Rv, pv_res], writes=[racc])
                                if (kt, n) == tiles[-1]:
                                    fin(qh, acc, racc)
                            pend.append({"fn": pv_job, "mask": mask_job, "tile": True})
                            flush_pv(keep=LA)

            def recip_rows(acc, racc, hh, then, on_dve=False):
                nr = slice(hh * 64, hh * 64 + 64)
                dr_ = slice((1 - hh) * 64, (1 - hh) * 64 + 64)

                def job():
                    if on_dve:
                        P.add("dve", lambda e: e.tensor_copy(bcs[nr, :], acc[dr_, :]), reads=[racc], writes=[Rbcs])
                        P.add("dve", lambda e: e.reciprocal(bcs[nr, :], bcs[nr, :]), reads=[Rbcs], writes=[Rbcs])
                    else:
                        P.add("act", lambda e: e.activation(bcs[nr, :], acc[dr_, :], AF.Ln), reads=[racc], writes=[Rbcs])
                        P.add("act", lambda e: e.activation(bcs[nr, :], bcs[nr, :], AF.Exp, scale=-1.0), reads=[Rbcs], writes=[Rbcs])
                    then()
                pend.append({"fn": job, "mask": None})

            def simple_fin(mb, hh, on_dve=False):
                nr = slice(hh * 64, hh * 64 + 64)

                def fin(qh, acc, racc):
                    def then():
                        P.add("dve", lambda e: e.tensor_tensor(
                            mixed[mb][nr, qh * 1024:(qh + 1) * 1024], acc[nr, :], bcs[nr, :], ALU.mult),
                            reads=[racc, Rbcs], writes=[Rmixed[mb]])
                    recip_rows(acc, racc, hh, then, on_dve=on_dve)
                return fin

            mB = A.mark()
            cosT = A.alloc(S * 4, F32, "cos")
            sinT = A.alloc(S * 4, F32, "sin")
            Rtab = Res("tab")
            wqc = A.alloc(3 * 8 * 128 * 2, BF16, "wqc").rearrange("p (c h e) -> p c h e", c=3, h=8)
            wukv = A.alloc(2 * 1024 * 2, BF16, "wukv").rearrange("p (c f) -> p c f", c=2)
            swb = dsem()
            pre_cq = load_w([(768, 384)])
            pre_ckv = load_w([(1152, 256)])
            uq4 = dr["mla_w_uq"][l].rearrange("(c p) (h e) -> p c h e", p=128, e=96)
            Rwb = []

            def wb_dma(out_ap, in_ap):
                r_ = Res("wb%d" % len(Rwb))
                Rwb.append(r_)
                P.add("pool", lambda e: e.dma_start(out=out_ap, in_=in_ap, allow_slow_non_contiguous=True),
                      writes=[r_], dma_sem=swb)

            wb_dma(wukv, dr["mla_w_ukv"][l].rearrange("(c p) f -> p c f", p=128))
            for c in range(3):
                wb_dma(wqc[:, c, :, 0:96], uq4[:, c, :, 0:96])
                wb_dma(wqc[:, c, :, 96:112], uq4[:, c, :, 80:96])
                wb_dma(wqc[:, c, :, 112:128], uq4[:, c, :, 64:80])
            m2 = A.mark()
            posi = pairbuf.bitcast(I32)[:, 0:S]
            posf = mixedall.bitcast(F32)[:, 0:S]
            Rpos = Res("pos")
            spos = dsem()
            P.add("sp", lambda e: e.dma_start(out=posi, in_=dr["positions"].partition_broadcast(128)),
                  writes=[Rpos], dma_sem=spos)
            P.add("dve", lambda e: e.tensor_copy(posf, posi), reads=[Rpos], writes=[Rpos])
            P.add("dve", lambda e: e.tensor_scalar(posf, posf, c_inv, None, ALU.mult), reads=[Rpos, Rcst], writes=[Rpos])
            TWO_PI = 2.0 * math.pi
            for (tab, shift) in ((cosT, 0.5 * math.pi), (sinT, 0.0)):
                P.add("dve", lambda e, tab=tab, shift=shift: e.tensor_scalar(tab, posf, shift, None, ALU.add),
                      reads=[Rpos], writes=[Rtab])
                P.add("dve", lambda e, tab=tab: e.tensor_scalar(posi, tab, 1.0 / TWO_PI, None, ALU.mult),
                      reads=[Rtab], writes=[Rscr])
                P.add("dve", lambda e: e.tensor_copy(bcrr, posi), reads=[Rscr], writes=[Rbcs, Rrrow])
                P.add("dve", lambda e, tab=tab: e.scalar_tensor_tensor(tab, bcrr, -TWO_PI, tab, ALU.mult, ALU.add),
                      reads=[Rbcs, Rrrow, Rtab], writes=[Rtab])
                P.add("dve", lambda e, tab=tab: e.tensor_scalar(bcrr, tab, math.pi, -TWO_PI, ALU.is_gt, ALU.mult),
                      reads=[Rtab], writes=[Rbcs, Rrrow])
                P.add("dve", lambda e, tab=tab: e.tensor_tensor(tab, tab, bcrr, ALU.add),
                      reads=[Rbcs, Rrrow, Rtab], writes=[Rtab])
                P.add("dve", lambda e, tab=tab: e.tensor_scalar(tab, tab, -math.pi, math.pi, ALU.max, ALU.min),
                      reads=[Rtab], writes=[Rtab])
                P.add("act", lambda e, tab=tab: e.activation(tab, tab, AF.Sin), reads=[Rtab], writes=[Rtab])
            P.add("dve", lambda e: e.tensor_scalar(sinT, sinT, c_sign, None, ALU.mult), reads=[Rtab, Rcst], writes=[Rtab])
            if debug and l == 0:
                sdd = dsem()
                ddump(P, "d_cos", cosT, [128, S], F32, [Rtab], sdd)
                ddump(P, "d_sin", sinT, [128, S], F32, [Rtab], sdd)
            P.barrier()
            A.release(m2)
            def latent_norm(pre, nchunk, gname, dst, Rdst, nfeat):
                wv, rw = pre
                sqb = [pairbuf[:, i * 1024:(i + 1) * 1024] for i in range(3)]
                RPT = [Rq[0], Rq[1], Rk[0]]
                for th in range(2):
                    slots = [next_slot() for _ in range(nchunk)]
                    for ci, (sl, rsl) in enumerate(slots):
                        for c in range(8):
                            for n in range(2):
                                tq = th * 2 + n
                                P.add("pe", lambda e, sl=sl, c=c, n=n, tq=tq, ci=ci: e.matmul(
                                    sl[:, n * 512:(n + 1) * 512], wv[:, c, ci * 128:(ci + 1) * 128],
                                    hT[:, c, tq * 512:(tq + 1) * 512], start=(c == 0), stop=(c == 7)),
                                    reads=[rw, RH[tq]], writes=[rsl])
                    ssl, rssl = next_slot()
                    for ci, (sl, rsl) in enumerate(slots):
                        P.add("act", lambda e, sl=sl, ci=ci: e.activation(sqb[ci], sl, AF.Square), reads=[rsl], writes=[RPT[ci]])
                        for n in range(2):
                            P.add("pe", lambda e, ssl=ssl, n=n, ci=ci: e.matmul(
                                ssl[:, n * 512:(n + 1) * 512], ones_bf, sqb[ci][:, n * 512:(n + 1) * 512],
                                start=(ci == 0), stop=(ci == nchunk - 1)), reads=[RPT[ci], Rconst], writes=[rssl])
                    P.add("act", lambda e, ssl=ssl: e.activation(bcs, ssl, AF.Ln, bias=c_eps, scale=1.0 / nfeat),
                          reads=[rssl, Rcst], writes=[Rbcs])
                    P.add("act", lambda e: e.activation(bcs, bcs, AF.Exp, scale=-0.5), reads=[Rbcs], writes=[Rbcs])
                    for ci, (sl, rsl) in enumerate(slots):
                        P.add("dve", lambda e, sl=sl, ci=ci, th=th: e.scalar_tensor_tensor(
                            dst[:, ci, th * 1024:(th + 1) * 1024], sl, cs(gname, 1, ci), bcs, ALU.mult, ALU.mult),
                            reads=[rsl, Rbcs, Rcst], writes=[Rdst[th]])

            latent_norm(pre_cq, 3, "gq_%d" % l, cqn, Rcqn, 384.0)
            latent_norm(pre_ckv, 2, "gkv_%d" % l, ckvn, Rckvn, 256.0)
            wv, rw = load_w([(1408, 32), (1424, 16), (1408, 16)])
            if debug and l == 0:
                ddump(P, "d_wkpe", wv, [128, 8, 64], BF16, [rw], sdd)
            for th in range(2):
                slA, rA = next_slot()
                slB, rB = next_slot()
                for (sl_, r_, c0_) in ((slA, rA, 0), (slB, rB, 32)):
                    for c in range(8):
                        for n in range(2):
                            tq = th * 2 + n
                            op_ = P.add("pe", lambda e, sl_=sl_, c=c, n=n, tq=tq, c0_=c0_, wv=wv: e.matmul(
                                sl_[0:32, n * 512:(n + 1) * 512], wv[:, c, c0_:c0_ + 32],
                                hT[:, c, tq * 512:(tq + 1) * 512], start=(c == 0), stop=(c == 7)),
                                reads=[rw, RH[tq]], writes=[r_])
                            if not hasattr(P, "kpe_first"):
                                P.kpe_first = op_.id
                tsl = slice(th * 1024, (th + 1) * 1024)
                if debug and l == 0 and th == 0:
                    dd_ = mixedall.bitcast(F32)[:, 0:1024]
                    Rdd = Res("dd")
                    P.add("act", lambda e, slA=slA: e.copy(dd_, slA), reads=[rA], writes=[Rdd])
                    ddump(P, "d_slA", dd_, [128, 1024], F32, [Rdd], sdd)
                P.add("dve", lambda e, slA=slA, tsl=tsl: e.scalar_tensor_tensor(
                    ropeA[64:96, :], slA[0:32, :], 1.0, cosT[0:32, tsl], ALU.mult, ALU.mult),
                    reads=[rA, Rtab], writes=[RropeA])
                P.add("dve", lambda e, slB=slB, tsl=tsl: e.scalar_tensor_tensor(
                    ropeB[64:96, :], slB[0:32, :], 1.0, sinT[0:32, tsl], ALU.mult, ALU.mult),
                    reads=[rB, Rtab], writes=[RropeB])
                P.add("dve", lambda e, tsl=tsl: e.tensor_tensor(krope[64:96, tsl], ropeA[64:96, :], ropeB[64:96, :], ALU.add),
                      reads=[RropeA, RropeB], writes=[Rkrope])
                P.add("dve", lambda e, tsl=tsl: e.tensor_tensor(krope[96:128, tsl], ropeA[64:96, :], ropeB[64:96, :], ALU.add),
                      reads=[RropeA, RropeB], writes=[Rkrope])
                if debug and l == 0 and th == 0:
                    ddump(P, "d_ropeA", ropeA, [128, 1024], F32, [RropeA], sdd)
                    ddump(P, "d_ropeB", ropeB, [128, 1024], F32, [RropeB], sdd)

            qh_t = [pairbuf[:, i * 2048:(i + 1) * 2048] for i in range(2)]
            kh_t = [pairbuf[:, 4096 + i * 2048:4096 + (i + 1) * 2048] for i in range(2)]
            sc_b = 96.0 ** -0.5
            for pr in range(4):
                mb = cnt["mix"] % 2
                cnt["mix"] += 1
                for hh in range(2):
                    h = 2 * pr + hh
                    for th in range(2):
                        slA, rA = next_slot()
                        for c in range(3):
                            for n in range(2):
                                tq = th * 2 + n
                                P.add("pe", lambda e, slA=slA, c=c, n=n, tq=tq, h=h: e.matmul(
                                    slA[:, n * 512:(n + 1) * 512], wqc[:, c, h, :],
                                    cqn[:, c, tq * 512:(tq + 1) * 512], start=(c == 0), stop=(c == 2)),
                                    reads=[Rwb, Rcqn[th]], writes=[rA])
                        tsl = slice(th * 1024, (th + 1) * 1024)
                        P.add("act", lambda e, slA=slA, hh=hh, tsl=tsl: e.activation(qh_t[hh][0:64, tsl], slA[0:64, :], AF.Copy, scale=sc_b),
                              reads=[rA], writes=[Rq[hh]])
                        P.add("dve", lambda e, slA=slA, tsl=tsl, hh=hh: e.scalar_tensor_tensor(
                            qh_t[hh][64:96, tsl], slA[64:96, :], sc_b, cosT[64:96, tsl], ALU.mult, ALU.mult),
                            reads=[rA, Rtab], writes=[Rq[hh]])
                        P.add("dve", lambda e, slA=slA, tsl=tsl, hh=hh: e.scalar_tensor_tensor(
                            qh_t[hh][96:128, tsl], slA[96:128, :], sc_b, sinT[96:128, tsl], ALU.mult, ALU.mult),
                            reads=[rA, Rtab], writes=[Rq[hh]])
                    proj_fm(wukv, Rwb, h * 128, 64, lambda th, hh=hh: kh_t[hh][0:64, th * 1024:(th + 1) * 1024],
                            lambda th, hh=hh: Rk[hh], 1.0, src=ckvn, rsrc=Rckvn, nck=2)
                    P.add("dve", lambda e, hh=hh: e.tensor_copy(kh_t[hh][64:128, :], krope[64:128, :]),
                          reads=[Rkrope], writes=[Rk[hh]])
                proj_v(wukv, Rwb, [(2 * pr) * 128 + 64, (2 * pr + 1) * 128 + 64], src=ckvn, rsrc=Rckvn, nck=2)
                wout_drain()
                if debug and l == 0 and pr == 0:
                    ddump(P, "d_qk", pairbuf[:, 0:8192], [128, 8192], BF16, [Rq[0], Rq[1], Rk[0], Rk[1]], sdd)
                    ddump(P, "d_v", pairbuf[:, 8192:8192 + 2112], [128, 2112], BF16, [Rv], sdd)
                    ddump(P, "d_cqn", cqn_flat, [128, 3 * S], BF16, Rcqn, sdd)
                for hh in range(2):
                    attend(kh_t[hh][0:128, :], Rk[hh], qh_t[hh][0:128, :], Rq[hh], hh, None, simple_fin(mb, hh, on_dve=True))
                wout_later(2 + pr, mb)
            CB = 1440
            wlist = [[(pr * 128, 128), (256 + pr * 128, 128), (512 + pr * 128, 128)] for pr in range(2)] + \
                    [[(CB + pr * 128, 128), (CB + 256 + pr * 128, 128), (CB + 512 + pr * 128, 128)] for pr in range(2)]
            wnext = [load_w(wlist[0])]
            wout_drain()
            P.barrier()
            A.release(mB)

            posq = A.alloc(S * 2, F16, "posq")
            posk = A.alloc(16 * 4, F32, "posk")
            augq = A.alloc(S * 2, BF16, "augq")
            augk = A.alloc(S * 2, BF16, "augk")
            Rpos = Res("pos2")
            Raug = Res("aug")
            saug_q = [dsem(), dsem()]
            saug_k = [dsem(), dsem()]
            m3 = A.mark()
            posi = pairbuf.bitcast(I32)[:, 0:S]
            posf = mixedall.bitcast(F32)[:, 0:S]
            phi = bcrr
            plo = cqn_flat.bitcast(F32)[:, 0:S]
            poski = A.alloc(16 * 4, I32, "poski")
            spos = dsem()
            Rt = Res("augtmp")
            P.add("sp", lambda e: e.dma_start(out=posi, in_=dr["positions"].partition_broadcast(128)),
                  writes=[Rpos], dma_sem=spos)
            P.add("sp", lambda e: e.dma_start(out=poski, in_=dr["positions"].rearrange("(t p) -> p t", p=128),
                                              allow_slow_non_contiguous=True), writes=[Rpos], dma_sem=spos)
            P.add("dve", lambda e: e.tensor_copy(posq, posi), reads=[Rpos], writes=[Rpos])
            P.add("dve", lambda e: e.tensor_copy(posk, poski), reads=[Rpos], writes=[Rpos])
            P.add("dve", lambda e: e.tensor_copy(posf, posi), reads=[Rpos], writes=[Rt])
            P.add("dve", lambda e: e.tensor_scalar(posi, posf, 1.0 / 64.0, None, ALU.mult), reads=[Rt], writes=[Rpos])
            P.add("dve", lambda e: e.tensor_copy(phi, posi), reads=[Rpos], writes=[Rt])
            P.add("dve", lambda e: e.tensor_scalar(phi, phi, 64.0, None, ALU.mult), reads=[Rt], writes=[Rt])
            P.add("dve", lambda e: e.tensor_tensor(plo, posf, phi, ALU.subtract), reads=[Rt], writes=[Rt])
            rcq = [cst[0:32, col["rc"] + i:col["rc"] + i + 1] for i in range(5, 11)]
            for (dst_, ca, cb, cc_) in ((augq, rcq[0], rcq[1], rcq[2]), (augk, rcq[3], rcq[4], rcq[5])):
                P.add("dve", lambda e, cb=cb, cc_=cc_: e.tensor_scalar(posf[0:32, :], plo[0:32, :], cb, cc_, ALU.mult, ALU.add),
                      reads=[Rt, Rcst], writes=[Rt])
                P.add("dve", lambda e, dst_=dst_, ca=ca: e.scalar_tensor_tensor(
                    dst_[0:32, :], phi[0:32, :], ca, posf[0:32, :], ALU.mult, ALU.add),
                    reads=[Rt, Rcst], writes=[Raug])
            P.barrier()
            A.release(m3)
            bias = {"posq": posq, "posk": posk, "rpos": Rpos}
            bias["posqh"] = A.alloc(S * 2, F16, "posqh")
            bias["poskh"] = A.alloc(16 * 4, F32, "poskh")
            bias["rposh"] = Res("posh")
            kr16 = krope.bitcast(F16)
            bias["dist"] = [kr16[:, i * 512:(i + 1) * 512] for i in range(4)]
            bias["rdist"] = [Res("dist%d" % i) for i in range(4)]
            cq32 = cqn_flat.bitcast(F32)
            bias["tmp"] = [cq32[:, i * 512:(i + 1) * 512] for i in range(4)]
            bias["rtmp"] = [Res("tmp%d" % i) for i in range(4)]
            ptm_all = A.alloc(2048 * 2, BF16, "ptm")
            bias["ptm"] = [ptm_all[:, i * 512:(i + 1) * 512] for i in range(4)]
            bias["rptm"] = [Res("ptm%d" % i) for i in range(4)]
            sqc = ptm_all[:, 0:1024]
            Rsqc = [bias["rptm"][0], bias["rptm"][1]]
            cmask = ckvn_flat
            Rmask = Res("cmask")
            smask = dsem()
            P.add("sp", lambda e: e.dma_start(out=cmask, in_=dr["c_mask"]), writes=[Rmask], dma_sem=smask)
            cm32 = cmask.bitcast(F32)
            o1 = cm32[:, 0:1024]
            o2 = cm32[:, 1024:2048]
            Ro = [Res("o1"), Res("o2")]
            qt = [pairbuf[:, i * 2048:(i + 1) * 2048] for i in range(2)]
            kt_ = [pairbuf[:, 4096 + i * 2048:4096 + (i + 1) * 2048] for i in range(2)]

            _hm = np.asarray(_host_consts()[1]).astype(np.float32)

            def mask_active(qh, n, kt):
                j0 = qh * 1024 + n * 512 - 128 * kt + 1920
                return bool(_hm[:, j0:j0 + 512].any())

            def put_aug(hh, row0, j):
                P.add("sp", lambda e, hh=hh, row0=row0, j=j: e.dma_start(out=qt[hh][row0:row0 + 4, :], in_=augq[4 * j:4 * j + 4, :]),
                      reads=[Raug], writes=[Rq[hh]], dma_sem=saug_q[hh])
                P.add("sp", lambda e, hh=hh, row0=row0, j=j: e.dma_start(out=kt_[hh][row0:row0 + 4, :], in_=augk[4 * j:4 * j + 4, :]),
                      reads=[Raug], writes=[Rk[hh]], dma_sem=saug_k[hh])

            def next_weights(i):
                cur = wnext.pop(0)
                if i + 1 < len(wlist):
                    wnext.append(load_w(wlist[i + 1]))
                return cur

            for pr in range(2):
                mb = cnt["mix"] % 2
                cnt["mix"] += 1
                wv, rw = next_weights(pr)
                for (dl, rl, wc0, scl) in ((qt, Rq, 0, 0.125), (kt_, Rk, 128, 1.0)):
                    for th in range(2):
                        sl, rsl = proj_psum(wv, rw, wc0, 128, th)
                        tsl = slice(th * 1024, (th + 1) * 1024)
                        for hh in range(2):
                            P.add("act", lambda e, dl=dl, sl=sl, tsl=tsl, scl=scl, hh=hh: e.activation(
                                dl[hh][0:64, tsl], sl[hh * 64:(hh + 1) * 64, :], AF.Copy, scale=scl),
                                reads=[rsl], writes=[rl[hh]])
                for hh in range(2):
                    put_aug(hh, 64, 2 * pr + hh)
                proj_v(wv, rw, [256, 320])
                wout_drain()
                for hh in range(2):
                    h = 2 * pr + hh
                    ba = dict(bias)
                    ba["slope"] = 2.0 ** (-(2 * h + 1))
                    ba["mask"] = cmask
                    ba["rmask"] = Rmask
                    ba["active"] = mask_active
                    attend(kt_[hh][0:68, :], Rk[hh], qt[hh][0:68, :], Rq[hh], hh, ba, simple_fin(mb, hh))
                wout_later(pr, mb)

            wout_drain()
            P.barrier()
            sc_c = 32.0 ** -0.5
            CB = 1440
            for pr in range(2):
                mb = cnt["mix"] % 2
                cnt["mix"] += 1
                wv, rw = next_weights(2 + pr)
                for (dl, rl, wc0, scl) in ((qt, Rq, 0, sc_c), (kt_, Rk, 128, 1.0)):
                    for th in range(2):
                        sl, rsl = proj_psum(wv, rw, wc0, 128, th)
                        tsl = slice(th * 1024, (th + 1) * 1024)
                        for hh in range(2):
                            for mp in range(2):
                                P.add("act", lambda e, dl=dl, sl=sl, tsl=tsl, scl=scl, hh=hh, mp=mp: e.activation(
                                    dl[hh][mp * 64:mp * 64 + 32, tsl], sl[hh * 64 + mp * 32:hh * 64 + mp * 32 + 32, :],
                                    AF.Copy, scale=scl), reads=[rsl], writes=[rl[hh]])
                for hh in range(2):
                    put_aug(hh, 32, 4 + 2 * pr + hh)
                    put_aug(hh, 96, 4 + 2 * pr + hh)
                proj_v(wv, rw, [256, 320])
                wout_drain()
                for hh in range(2):
                    h = 2 * pr + hh
                    bc_ = dict(bias)
                    bc_["slope"] = 2.0 ** (-(2 * h + 2))
                    bc_["mask"] = None

                    def fin_map(mp, hh=hh, mb=mb):
                        nr = slice(hh * 64, hh * 64 + 64)

                        def fin(qh, acc, racc):
                            recip_rows(acc, racc, hh, lambda: fin_rest(qh, acc, racc))

                        def fin_rest(qh, acc, racc):
                            od = o1 if mp == 0 else o2
                            P.add("dve", lambda e: e.tensor_tensor(od[nr, :], acc[nr, :], bcs[nr, :], ALU.mult),
                                  reads=[racc, Rbcs], writes=[Ro[mp]])
                            if mp == 1:
                                P.add("dve", lambda e: e.scalar_tensor_tensor(
                                    o1[nr, :], o2[nr, :], cst[nr, col["nlam_%d" % l]:col["nlam_%d" % l] + 1],
                                    o1[nr, :], ALU.mult, ALU.add), reads=[Ro[0], Ro[1], Rcst], writes=[Ro[0]])
                                P.add("dve", lambda e: e.tensor_tensor(sqc[nr, :], o1[nr, :], o1[nr, :], ALU.mult),
                                      reads=[Ro[0]], writes=[Rsqc])
                                pend.append({"fn": lambda: fin_norm(qh), "mask": None})

                        def fin_norm(qh):
                            for n in range(2):
                                sb, rsb = s_bank()
                                P.add("pe", lambda e, sb=sb, n=n: e.matmul(
                                    sb, ones_bf[nr, :], sqc[nr, n * 512:(n + 1) * 512], start=True, stop=True),
                                    reads=[Rsqc, Rconst], writes=[rsb])
                                P.add("act", lambda e, sb=sb, n=n: e.activation(
                                    o2[nr, n * 512:(n + 1) * 512], sb[nr, :], AF.Ln,
                                    bias=cst[nr, col["rc"] + 3:col["rc"] + 4], scale=1.0 / 64.0),
                                    reads=[rsb, Rcst], writes=[Ro[1]])
                            P.add("act", lambda e: e.activation(o2[nr, :], o2[nr, :], AF.Exp, scale=-0.5), reads=[Ro[1]], writes=[Ro[1]])
                            P.add("dve", lambda e: e.scalar_tensor_tensor(
                                mixed[mb][nr, qh * 1024:(qh + 1) * 1024], o1[nr, :],
                                cst[nr, col["gc_%d" % l]:col["gc_%d" % l] + 1], o2[nr, :], ALU.mult, ALU.mult),
                                reads=[Ro[0], Ro[1], Rcst], writes=[Rmixed[mb]])
                        return fin

                    for qh in range(2):
                        for mp in range(2):
                            attend(kt_[hh][mp * 64:mp * 64 + 36, :], Rk[hh], qt[hh][mp * 64:mp * 64 + 36, :], Rq[hh],
                                   hh, bc_, fin_map(mp), qhs=(qh,))
                wout_later(6 + pr, mb)
            wout_drain()
            P.barrier()
            A.release(m)

        def final():
            m = A.mark()
            gfin = A.alloc(D * 4, F32, "gfin")
            Rg = Res("gfin")
            sg_ = dsem()
            P.add("sp", lambda e: e.dma_start(out=gfin, in_=dr["final_norm"].partition_broadcast(128)),
                  writes=[Rg], dma_sem=sg_)
            yb = [A.alloc(D * 4, F32, "y%d" % i) for i in range(2)]
            Ry = [Res("y%d" % i) for i in range(2)]
            sy = [dsem(), dsem()]
            junk = A.alloc(D * 4, F32, "junk")
            Rjunk = Res("junk")
            ssq = A.alloc(32 * 4, F32, "ssq")
            Rssq = Res("ssq")
            for tt in range(16):
                h = tt // 8
                b = tt % 2
                sl, rsl = next_slot()
                for c in range(8):
                    P.add("pe", lambda e, sl=sl, c=c, tt=tt: e.transpose(sl[:, c * 128:(c + 1) * 128],
                                                                        xT[:, c, tt * 128:(tt + 1) * 128], ident),
                          reads=[RX[c][h], Rconst], writes=[rsl])
                sq_ = ssq[:, tt:tt + 1]
                P.add("act", lambda e, sl=sl, sq_=sq_: e.activation(junk, sl, AF.Square, accum_out=sq_),
                      reads=[rsl], writes=[Rjunk, Rssq])
                P.add("act", lambda e, sq_=sq_: e.activation(sq_, sq_, AF.Sqrt, bias=c_eps, scale=1.0 / D),
                      reads=[Rssq, Rcst], writes=[Rssq])
                P.add("dve", lambda e, sq_=sq_: e.reciprocal(sq_, sq_), reads=[Rssq], writes=[Rssq])
                P.add("dve", lambda e, sl=sl, sq_=sq_, b=b: e.scalar_tensor_tensor(yb[b], sl, sq_, gfin, ALU.mult, ALU.mult),
                      reads=[rsl, Rssq, Rg], writes=[Ry[b]])
                P.add("sp", lambda e, b=b, tt=tt: e.dma_start(out=out[tt * 128:(tt + 1) * 128, :], in_=yb[b]),
                      reads=[Ry[b]], dma_sem=sy[b])
            P.barrier()
            A.release(m)

        stages = []
        for l in range(DEPTH):
            stages += [("ffn", l, 1), ("mix", l, 0), ("ffn", l, 2)]
        nst = len(stages) if stop is None else stop
        for (kind, l, w) in stages[:nst]:
            if kind == "ffn":
                ffn(l, w)
            else:
                mixer(l)
        if debug:
            sd = dsem()
            P.add("sp", lambda e: e.dma_start(out=dbg, in_=xT), reads=[RX[c][h] for c in range(8) for h in range(2)],
                  dma_sem=sd)
            P.barrier()
        final()
        print("[kernel] ops=%d arena_peak=%d bytes dsems=%d" % (len(P.ops), A.peak * 4, dsem_i[0]))
        global _LASTP
        _LASTP = P
        P.emit(sems)
    return nc


_CACHE = {}


def kernel(**inputs):
    ident, cmask, rc = _host_consts()
    nc = build()
    shared = {k: np.ascontiguousarray(np.asarray(inputs[k], dtype=np.float32)) for k in PARAM_SHAPES}
    x = np.asarray(inputs["x"], dtype=np.float32)
    pos = np.asarray(inputs["positions"]).astype(np.int32)
    in_maps = []
    for b in range(8):
        m = dict(shared)
        m["x"] = np.ascontiguousarray(x[b])
        m["positions"] = np.ascontiguousarray(pos[b])
        m["c_ident"] = ident
        m["c_mask"] = cmask
        m["c_rc"] = rc
        in_maps.append(m)
    res = run_bass_kernel_spmd(nc, in_maps, core_ids=list(range(8)))
    return np.stack([np.asarray(r["out"], dtype=np.float32) for r in res.results], axis=0)
```
